# Optimizing a Trainium2 kernel written in Bass

```python
import jax, jax.numpy as jnp
from jax import lax
import numpy as np

D_MODEL = 2048
BATCH = 1
SEQ = 16384
DEPTH = 4
DEC_BATCH = 32
DEC_SEQ = 32
PAST_LEN = 2048

CHUNK = 64
WINDOW = 128
WIN_CHUNKS = WINDOW // CHUNK
N_A_LAYERS = DEPTH // 2
N_B_LAYERS = DEPTH - N_A_LAYERS
CONV_W = 3
D_FF = 4 * D_MODEL
HEAD_DIM = 64
N_HEADS = D_MODEL // HEAD_DIM
N_KV_HEADS = N_HEADS // 8
GROUP = N_HEADS // N_KV_HEADS
ROT_DIM = HEAD_DIM // 4
ROPE_THETA = 500000.0
EPS = 1e-6
SCALE = HEAD_DIM ** -0.5

kernel_name = 'yoco_shortconv_swa_sink_stream_step'


def rms_norm(x, g):
    xf = x.astype(jnp.float32)
    y = xf * lax.rsqrt(jnp.mean(xf * xf, axis=-1, keepdims=True) + EPS)
    return (y * g.astype(jnp.float32)).astype(x.dtype)


def rope_partial(x, pos):
    half = ROT_DIM // 2
    inv = ROPE_THETA ** (-jnp.arange(half, dtype=jnp.float32) / half)
    ang = pos.astype(jnp.float32)[:, None] * inv[None, :]
    cos = jnp.cos(ang)[None, :, None, :]
    sin = jnp.sin(ang)[None, :, None, :]
    xr = x[..., :ROT_DIM].astype(jnp.float32)
    x1, x2 = xr[..., :half], xr[..., half:]
    rot = jnp.concatenate([x1 * cos - x2 * sin, x2 * cos + x1 * sin], axis=-1)
    return jnp.concatenate([rot.astype(x.dtype), x[..., ROT_DIM:]], axis=-1)


def squared_relu_mlp(h, w_up, w_down):
    return jnp.square(jax.nn.relu(h @ w_up)) @ w_down


def short_conv_mixer(h, w_in, conv_w, w_out, prev):
    s = h.shape[1]
    gate_b, gate_c, u = jnp.split(h @ w_in, 3, axis=-1)
    z = gate_c * u
    zp = jnp.concatenate([prev.astype(z.dtype), z], axis=1)
    conv = zp[:, 0:s] * conv_w[0]
    for j in range(1, CONV_W):
        conv = conv + zp[:, j:j + s] * conv_w[j]
    return (gate_b * conv) @ w_out, zp[:, -(CONV_W - 1):]


def shared_kv(x, kv_norm_g, w_kv, k_norm_g, pos):
    b, s, _ = x.shape
    k, v = jnp.split(rms_norm(x, kv_norm_g) @ w_kv, 2, axis=-1)
    k = rope_partial(rms_norm(k.reshape(b, s, N_KV_HEADS, HEAD_DIM), k_norm_g), pos)
    return k, v.reshape(b, s, N_KV_HEADS, HEAD_DIM)


def queries(h, w_q, q_norm_g, pos):
    b, s, _ = h.shape
    q = (h @ w_q).reshape(b, s, N_HEADS, HEAD_DIM)
    return rope_partial(rms_norm(q, q_norm_g), pos)


def sink_softmax(sc, sink, valid):
    sc = jnp.where(valid, sc, -jnp.inf)
    m = jnp.maximum(jnp.max(sc, axis=-1, keepdims=True), sink)
    e = jnp.exp(sc - m)
    return e / (jnp.sum(e, axis=-1, keepdims=True) + jnp.exp(sink - m))


def window_attn_prompt(q, k, v, sinks):
    b, s = q.shape[:2]
    nc = s // CHUNK
    qc = q.reshape(b, nc, CHUNK, N_KV_HEADS, GROUP, HEAD_DIM)

    def band(t):
        tc = t.reshape(b, nc, CHUNK, N_KV_HEADS, HEAD_DIM)
        pad = jnp.zeros((b, WIN_CHUNKS, CHUNK, N_KV_HEADS, HEAD_DIM), t.dtype)
        tp = jnp.concatenate([pad, tc], axis=1)
        return jnp.concatenate([tp[:, j:j + nc] for j in range(WIN_CHUNKS + 1)], axis=2)

    kb, vb = band(k), band(v)
    sc = jnp.einsum('bnqhgd,bnkhd->bnhgqk', qc, kb).astype(jnp.float32) * SCALE
    key_chunk = (jnp.arange(nc)[:, None] - WIN_CHUNKS
                 + jnp.repeat(jnp.arange(WIN_CHUNKS + 1), CHUNK)[None, :])
    valid = (key_chunk >= 0)[None, :, None, None, None, :]
    sink = sinks.astype(jnp.float32).reshape(N_KV_HEADS, GROUP)[None, None, :, :, None, None]
    p = sink_softmax(sc, sink, valid).astype(vb.dtype)
    o = jnp.einsum('bnhgqk,bnkhd->bnqhgd', p, vb)
    return o.reshape(b, s, N_HEADS * HEAD_DIM)


def window_attn_sample(q, k_all, v_all, sinks):
    b, s = q.shape[:2]
    qg = q.reshape(b, s, N_KV_HEADS, GROUP, HEAD_DIM)
    sc = jnp.einsum('bqhgd,bkhd->bhgqk', qg, k_all).astype(jnp.float32) * SCALE
    sink = sinks.astype(jnp.float32).reshape(N_KV_HEADS, GROUP)[None, :, :, None, None]
    p = sink_softmax(sc, sink, True).astype(v_all.dtype)
    o = jnp.einsum('bhgqk,bkhd->bqhgd', p, v_all)
    return o.reshape(b, s, N_HEADS * HEAD_DIM)


def setup_inputs(seed: int = 0) -> dict:
    key = jax.random.key(seed)
    ks = jax.random.split(key, 19)
    f32 = jnp.float32

    def nrm(k, shape, scale):
        return jax.random.normal(k, shape, f32) * scale

    def gain(k, shape):
        return 1.0 + 0.05 * jax.random.normal(k, shape, f32)

    return {
        'x_prompt': nrm(ks[0], (BATCH, SEQ, D_MODEL), 1.0),
        'x_sample': nrm(ks[1], (DEC_BATCH, DEC_SEQ, D_MODEL), 1.0),
        'state_conv': nrm(ks[2], (N_A_LAYERS, DEC_BATCH, CONV_W - 1, D_MODEL), 1.0),
        'cache_k': nrm(ks[3], (DEC_BATCH, WINDOW, N_KV_HEADS, HEAD_DIM), 1.0),
        'cache_v': nrm(ks[4], (DEC_BATCH, WINDOW, N_KV_HEADS, HEAD_DIM), 1.0),
        'mix_norm_g': gain(ks[5], (DEPTH, D_MODEL)),
        'mlp_norm_g': gain(ks[6], (DEPTH, D_MODEL)),
        'w_up': nrm(ks[7], (DEPTH, D_MODEL, D_FF), D_MODEL ** -0.5),
        'w_down': nrm(ks[8], (DEPTH, D_FF, D_MODEL), D_FF ** -0.5),
        'conv_w_in': nrm(ks[9], (N_A_LAYERS, D_MODEL, 3 * D_MODEL), D_MODEL ** -0.5),
        'conv_w': nrm(ks[10], (N_A_LAYERS, CONV_W, D_MODEL), CONV_W ** -0.5),
        'conv_w_out': nrm(ks[11], (N_A_LAYERS, D_MODEL, D_MODEL), D_MODEL ** -0.5),
        'kv_norm_g': gain(ks[12], (D_MODEL,)),
        'w_kv': nrm(ks[13], (D_MODEL, 2 * N_KV_HEADS * HEAD_DIM), D_MODEL ** -0.5),
        'k_norm_g': gain(ks[14], (HEAD_DIM,)),
        'w_q': nrm(ks[15], (N_B_LAYERS, D_MODEL, N_HEADS * HEAD_DIM), D_MODEL ** -0.5),
        'q_norm_g': gain(ks[16], (N_B_LAYERS, HEAD_DIM)),
        'sinks': nrm(ks[17], (N_B_LAYERS, N_HEADS), 0.5),
        'w_o': nrm(ks[18], (N_B_LAYERS, N_HEADS * HEAD_DIM, D_MODEL), (N_HEADS * HEAD_DIM) ** -0.5),
    }


def reference(x_prompt, x_sample, state_conv, cache_k, cache_v,
              mix_norm_g, mlp_norm_g, w_up, w_down,
              conv_w_in, conv_w, conv_w_out,
              kv_norm_g, w_kv, k_norm_g,
              w_q, q_norm_g, sinks, w_o):
    pos_p = jnp.arange(x_prompt.shape[1])
    pos_s = PAST_LEN + jnp.arange(x_sample.shape[1])
    xp, xs = x_prompt, x_sample
    prev_p = jnp.zeros((x_prompt.shape[0], CONV_W - 1, D_MODEL), x_prompt.dtype)
    conv_p, conv_s = [], []
    for i in range(DEPTH):
        if i < N_A_LAYERS:
            yp, cp = short_conv_mixer(rms_norm(xp, mix_norm_g[i]), conv_w_in[i], conv_w[i],
                                      conv_w_out[i], prev_p)
            ys, cs = short_conv_mixer(rms_norm(xs, mix_norm_g[i]), conv_w_in[i], conv_w[i],
                                      conv_w_out[i], state_conv[i])
            conv_p.append(cp)
            conv_s.append(cs)
        else:
            if i == N_A_LAYERS:
                kp, vp = shared_kv(xp, kv_norm_g, w_kv, k_norm_g, pos_p)
                ks_new, vs_new = shared_kv(xs, kv_norm_g, w_kv, k_norm_g, pos_s)
                ks_all = jnp.concatenate([cache_k.astype(ks_new.dtype), ks_new], axis=1)
                vs_all = jnp.concatenate([cache_v.astype(vs_new.dtype), vs_new], axis=1)
            j = i - N_A_LAYERS
            qp = queries(rms_norm(xp, mix_norm_g[i]), w_q[j], q_norm_g[j], pos_p)
            yp = window_attn_prompt(qp, kp, vp, sinks[j]) @ w_o[j]
            qs = queries(rms_norm(xs, mix_norm_g[i]), w_q[j], q_norm_g[j], pos_s)
            ys = window_attn_sample(qs, ks_all, vs_all, sinks[j]) @ w_o[j]
        xp = xp + yp
        xs = xs + ys
        xp = xp + squared_relu_mlp(rms_norm(xp, mlp_norm_g[i]), w_up[i], w_down[i])
        xs = xs + squared_relu_mlp(rms_norm(xs, mlp_norm_g[i]), w_up[i], w_down[i])
    return (xp, xs, jnp.stack(conv_p), jnp.stack(conv_s),
            kp[:, -WINDOW:], vp[:, -WINDOW:], ks_all[:, -WINDOW:], vs_all[:, -WINDOW:])
```

```python
import numpy as np
import concourse.bass as bass
import concourse.mybir as mybir
from concourse.bass_utils import run_bass_kernel_spmd
from contextlib import ExitStack

F32, BF16 = mybir.dt.float32, mybir.dt.bfloat16
ALU = mybir.AluOpType
AF = mybir.ActivationFunctionType


class Cfg:
    def __init__(self, D=2048, PC=2048, pass_prompt=(448, 576, 576, 448), PAST=2048, NCORES=8, NSEQ=4):
        self.D = D; self.KD = D // 128; self.FF = 4 * D; self.KF = self.FF // 128
        self.NH = D // 64; self.NG = self.NH // 8
        self.PC = PC; self.NSEQ = NSEQ; self.SL = 32; self.HALO = 132; self.PAST = PAST
        self.pass_prompt = list(pass_prompt); self.NP = len(pass_prompt); self.NCORES = NCORES
        self.EPS = 1e-6; self.THETA = 500000.0; self.LA = 2; self.LB = 2; self.L = 4
        self.NSLOT = 6
        assert sum(pass_prompt) == PC and all(n % 64 == 0 for n in pass_prompt)
        self.pc0 = []; self.T = []; self.sc0 = []
        for p, n in enumerate(self.pass_prompt):
            pc0 = 2 + (self.HALO if p == 0 else 0)
            T = pc0 + n
            sc = []
            if p == self.NP - 1:
                for s in range(NSEQ):
                    sc.append(T + 2); T += 2 + self.SL
            self.pc0.append(pc0); self.T.append(T); self.sc0.append(sc)
        self.TMAX = max(self.T)
        self.R0 = [int(v) for v in np.cumsum([0] + self.T)]
        self.O0 = [int(v) for v in np.cumsum([0] + [t - c for t, c in zip(self.T, self.pc0)])]
        KD, KF = self.KD, self.KF
        self.HU = KF // 2 // KD
        self.NU = self.LA * (3 * KD + KD + 2 * (KF // 2 + KD * self.HU)) + 2 * self.NG + \
            self.LB * (2 * KD + 2 * (KF // 2 + KD * self.HU))
        o = 0; self.off = {}
        def add(name, n):
            nonlocal o
            self.off[name] = o; o += n
        for l in range(self.L):
            add(f"mixg{l}", KD); add(f"mlpg{l}", KD)
        add("kvg", KD)
        for l in range(self.LA):
            for t in range(3):
                add(f"cw{l}_{t}", KD)
        for lb in range(self.LB):
            add(f"gq{lb}", 1)
        add("gk", 1)
        add("sinks", self.LB * self.NG * 8)
        add("kmask", 2)
        self.NPC = o


def _unit(W, k0, n0, KD, dup64=False):
    if dup64:
        blk = W[k0:k0 + KD * 128, n0:n0 + 64]
        blk = np.concatenate([blk, blk], axis=1)
    else:
        blk = W[k0:k0 + KD * 128, n0:n0 + 128]
    return blk.reshape(KD, 128, 128).transpose(1, 0, 2).reshape(128, KD * 128)


def build_wall(cfg, inp):
    KD, KF, D, NG = cfg.KD, cfg.KF, cfg.D, cfg.NG
    wall = np.empty((cfg.NU, 128, KD * 128), np.float32)
    u = 0
    def put(a):
        nonlocal u
        wall[u] = a; u += 1
    def mlp(l):
        for half in range(2):
            for f in range(half * KF // 2, (half + 1) * KF // 2):
                put(_unit(inp["w_up"][l], 0, f * 128, KD))
            for n in range(KD):
                for uu in range(cfg.HU):
                    put(_unit(inp["w_down"][l], half * (cfg.FF // 2) + uu * KD * 128, n * 128, KD))
    for l in range(cfg.LA):
        for j in range(KD):
            for part in range(3):
                put(_unit(inp["conv_w_in"][l], 0, part * D + j * 128, KD))
        for n in range(KD):
            put(_unit(inp["conv_w_out"][l], 0, n * 128, KD))
        mlp(l)
    for g in range(NG):
        put(_unit(inp["w_kv"], 0, g * 64, KD, dup64=True))
    for g in range(NG):
        put(_unit(inp["w_kv"], 0, NG * 64 + g * 64, KD, dup64=True))
    for lb in range(cfg.LB):
        for n in range(KD):
            put(_unit(inp["w_q"][lb], 0, n * 128, KD))
        for n in range(KD):
            put(_unit(inp["w_o"][lb], 0, n * 128, KD))
        mlp(cfg.LA + lb)
    assert u == cfg.NU
    return wall


def _fm(vec, KD):
    return np.ascontiguousarray(vec.reshape(KD, 128).T)


def host_prepare(cfg, inp):
    KD, D, NG, NSEQ = cfg.KD, cfg.D, cfg.NG, cfg.NSEQ
    wall = build_wall(cfg, inp)
    cm = np.zeros((128, 5, 128), np.float32)
    cm[:, 0, :] = np.eye(128)
    cm[:, 1, :] = 1.0 / D
    cm[0:64, 2, 0:64] = 1.0 / 64; cm[64:128, 2, 64:128] = 1.0 / 64
    for hb in (0, 64):
        for d in range(8):
            cm[hb + d + 8, 3, hb + d] = -1.0
            cm[hb + d, 3, hb + d + 8] = 1.0
    cm[:, 4, :] = 1.0
    cm = cm.reshape(128, 5 * 128)
    half = 8
    inv = (np.float32(cfg.THETA) ** (-np.arange(half, dtype=np.float32) / np.float32(half))).astype(np.float32)
    maps = []
    xp = inp["x_prompt"][0]
    for c in range(cfg.NCORES):
        base = c * cfg.PC
        xin = np.zeros((cfg.R0[-1], D), np.float32)
        pos = np.zeros((cfg.NP, cfg.TMAX), np.float32)
        tok = 0
        for p in range(cfg.NP):
            r0 = cfg.R0[p]
            if p == 0:
                lo = base - cfg.HALO
                if lo >= 0:
                    xin[r0 + 2:r0 + 2 + cfg.HALO] = xp[lo:base]
                pos[p, 2:2 + cfg.HALO] = np.arange(lo, base)
            n = cfg.pass_prompt[p]
            xin[r0 + cfg.pc0[p]:r0 + cfg.pc0[p] + n] = xp[base + tok:base + tok + n]
            pos[p, cfg.pc0[p]:cfg.pc0[p] + n] = np.arange(base + tok, base + tok + n)
            tok += n
            for s, c0 in enumerate(cfg.sc0[p]):
                xin[r0 + c0:r0 + c0 + cfg.SL] = inp["x_sample"][c * NSEQ + s]
                pos[p, c0:c0 + cfg.SL] = cfg.PAST + np.arange(cfg.SL)
        pos = pos.astype(np.float32)
        ang = pos[:, None, :] * inv[None, :, None]
        cosv, sinv = np.cos(ang).astype(np.float32), np.sin(ang).astype(np.float32)
        rope = np.zeros((cfg.NP, 2, 128, cfg.TMAX), np.float32)
        rope[:, 0] = 1.0
        for hb in (0, 64):
            rope[:, 0, hb:hb + 8] = cosv; rope[:, 0, hb + 8:hb + 16] = cosv
            rope[:, 1, hb:hb + 8] = sinv; rope[:, 1, hb + 8:hb + 16] = sinv
        pc = np.zeros((128, cfg.NPC), np.float32)
        o = cfg.off
        for l in range(cfg.L):
            pc[:, o[f"mixg{l}"]:o[f"mixg{l}"] + KD] = _fm(inp["mix_norm_g"][l], KD)
            pc[:, o[f"mlpg{l}"]:o[f"mlpg{l}"] + KD] = _fm(inp["mlp_norm_g"][l], KD)
        pc[:, o["kvg"]:o["kvg"] + KD] = _fm(inp["kv_norm_g"], KD)
        for l in range(cfg.LA):
            for t in range(3):
                pc[:, o[f"cw{l}_{t}"]:o[f"cw{l}_{t}"] + KD] = _fm(inp["conv_w"][l, t], KD)
        for lb in range(cfg.LB):
            pc[:, o[f"gq{lb}"]] = np.tile(inp["q_norm_g"][lb], 2)
        pc[:, o["gk"]] = np.tile(inp["k_norm_g"], 2)
        sk = inp["sinks"].reshape(cfg.LB, NG, 4, 2).transpose(0, 1, 3, 2).reshape(-1)
        pc[:, o["sinks"]:o["sinks"] + sk.size] = sk[None, :]
        if c == 0:
            pc[:, o["kmask"]] = -30000.0
            pc[0:64, o["kmask"] + 1] = -30000.0
        sc = inp["state_conv"][:, c * NSEQ:(c + 1) * NSEQ]
        scv = sc.reshape(cfg.LA, NSEQ, 2, KD, 128).transpose(4, 0, 3, 1, 2).reshape(128, -1)
        ck = inp["cache_k"][c * NSEQ:(c + 1) * NSEQ]
        cv = inp["cache_v"][c * NSEQ:(c + 1) * NSEQ]
        ckd = np.stack([ck, ck], axis=3).reshape(NSEQ, 128, NG * 128)
        cvd = np.stack([cv, cv], axis=3).reshape(NSEQ, 128, NG * 128)
        maps.append({"xin": xin, "wall": wall, "pcols": pc, "scv": np.ascontiguousarray(scv), "cmat": cm,
                     "rope": rope, "ckd": np.ascontiguousarray(ckd), "cvd": np.ascontiguousarray(cvd)})
    return maps


INTERLEAVE_QA = True


class Tok:
    __slots__ = ("sem", "key", "val", "eng")
    def __init__(self, sem, key, val, eng):
        self.sem = sem; self.key = key; self.val = val; self.eng = eng


class Res:
    __slots__ = ("w", "r", "name")
    def __init__(self, name=""):
        self.w = None; self.r = {}; self.name = name


class Eng:
    def __init__(self, name, h, sem):
        self.name = name; self.h = h; self.sem = sem; self.cnt = 0; self.seen = {}; self.key = "e_" + name


class DSem:
    def __init__(self, name, sem):
        self.sem = sem; self.cnt = 0; self.key = "d_" + name


def ctiles(c0, c1):
    out = []
    c = c0
    while c < c1:
        n = min(512, c1 - c)
        out.append((c, n)); c += n
    return out


def build_program(cfg):
    nc = bass.Bass("TRN2", target_bir_lowering=False)
    D, KD, KF, NG, NSEQ, SL, TMAX, NP = cfg.D, cfg.KD, cfg.KF, cfg.NG, cfg.NSEQ, cfg.SL, cfg.TMAX, cfg.NP
    KA = max(KF // 2, 2 * KD)
    LA, LB = cfg.LA, cfg.LB
    off = cfg.off
    NV = 2 + max(cfg.pass_prompt) // 64 + NSEQ
    es = ExitStack()
    def dram(name, shape, kind):
        return nc.dram_tensor(name, list(shape), F32, kind=kind).ap()
    xin = dram("xin", [cfg.R0[-1], D], "ExternalInput")
    wall = dram("wall", [cfg.NU, 128, KD * 128], "ExternalInput")
    pcols_d = dram("pcols", [128, cfg.NPC], "ExternalInput")
    scv_d = dram("scv", [128, LA * KD * NSEQ * 2], "ExternalInput")
    cmat_d = dram("cmat", [128, 5 * 128], "ExternalInput")
    rope_d = dram("rope", [NP, 2, 128, TMAX], "ExternalInput")
    ckd_d = dram("ckd", [NSEQ, 128, NG * 128], "ExternalInput")
    cvd_d = dram("cvd", [NSEQ, 128, NG * 128], "ExternalInput")
    yout = dram("yout", [cfg.O0[-1], D], "ExternalOutput")
    zout_d = dram("zout", [LA, 1 + NSEQ, 2, D], "ExternalOutput")
    ckp_d = dram("ckp", [128, NG * 64], "ExternalOutput")
    cvp_d = dram("cvp", [128, NG * 64], "ExternalOutput")
    cks_d = dram("cks", [NSEQ, 128, NG * 64], "ExternalOutput")
    cvs_d = dram("cvs", [NSEQ, 128, NG * 64], "ExternalOutput")

    def sb(name, shape, dt=F32):
        return es.enter_context(nc.sbuf_tensor("sb_" + name, list(shape), dt))
    xT = sb("xT", [128, KD, TMAX]); hT = sb("hT", [128, KD, TMAX], BF16); aT = sb("aT", [128, KA, TMAX], BF16)
    wsl = [sb(f"w{i}", [128, KD, 128], BF16) for i in range(cfg.NSLOT)]
    NTMP = 7
    tmp = [sb(f"tmp{i}", [128, TMAX + 4]) for i in range(NTMP)]
    rstd = sb("rstd", [128, TMAX]); rstd2 = sb("rstd2", [128, TMAX])
    cosT = sb("cosT", [128, TMAX]); sinT = sb("sinT", [128, TMAX])
    KW = 128 + TMAX
    kTa = sb("kTa", [128, NG, KW], BF16); vTa = sb("vTa", [128, NG, KW], BF16)
    Vt = sb("Vt", [128, NV, NG * 128], BF16)
    ckT = sb("ckT", [128, NSEQ, NG, 128], BF16); cvS = sb("cvS", [128, NSEQ, NG * 128], BF16)
    NE = 4
    Eb = [sb(f"E{i}", [128, 512], BF16) for i in range(NE)]
    Rr = sb("Rr", [128, 512]); hmb = sb("hmb", [128, 2, 128], BF16)
    KH = min(8, KD)
    stg = [sb(f"stg{i}", [128, KH * 128]) for i in range(2)]
    cm = sb("cm", [128, 5, 128]); cmb = sb("cmb", [128, 5, 128], BF16)
    pcs = sb("pcs", [128, cfg.NPC]); esk = sb("esk", [128, LB * NG * 8]); eskb = sb("eskb", [1, NG * 8 * 64], BF16)
    scv = sb("scv", [128, LA, KD, NSEQ, 2]); zprev = sb("zprev", [128, LA, KD, 2]); zo = sb("zo", [128, LA, KD, 1 + NSEQ, 2])
    epst = sb("epst", [128, 1]); epsc = epst[:, 0:1]
    kst = sb("kst", [128, NG, 64])
    ckf = stg[0][:, 0:NG * 128]
    pst = [es.enter_context(nc.psum_tensor(f"ps{i}", [128, 1024], F32)) for i in range(4)]

    def sem(name):
        return es.enter_context(nc.semaphore(name))
    PE = Eng("pe", nc.tensor, sem("s_pe")); ACT = Eng("act", nc.scalar, sem("s_act"))
    DVE = Eng("dve", nc.vector, sem("s_dve")); POOL = Eng("pool", nc.gpsimd, sem("s_pool")); SP = Eng("sp", nc.sync, sem("s_sp"))
    dsems = {}
    def dsem(name):
        if name not in dsems:
            dsems[name] = DSem(name, sem("d_" + name))
        return dsems[name]

    def sync_for(eng, reads, writes):
        toks = []
        for R in reads:
            if R.w is not None:
                toks.append(R.w)
        for R in writes:
            if R.w is not None:
                toks.append(R.w)
            toks.extend(R.r.values())
        for t in toks:
            if t.eng is eng and eng.name == "pe":
                continue
            if eng.seen.get(t.key, 0) >= t.val:
                continue
            eng.h.wait_ge(t.sem, t.val)
            eng.seen[t.key] = t.val

    def mark(tok, reads, writes):
        for R in reads:
            o = R.r.get(tok.key)
            if o is None or o.val < tok.val:
                R.r[tok.key] = tok
        for R in writes:
            R.w = tok; R.r = {}

    def op(eng, fn, reads, writes):
        sync_for(eng, reads, writes)
        ins = fn(eng.h)
        eng.cnt += 1
        ins.then_inc(eng.sem, 1)
        mark(Tok(eng.sem, eng.key, eng.cnt, eng), reads, writes)

    def dma(eng, ds, out, in_, reads, writes, **kw):
        sync_for(eng, reads, writes)
        ins = eng.h.dma_start(out=out, in_=in_, **kw)
        ds.cnt += 16
        ins.then_inc(ds.sem, 16)
        mark(Tok(ds.sem, ds.key, ds.cnt, None), reads, writes)

    r_x = [Res(f"x{k}") for k in range(KD)]; r_h = [Res(f"h{k}") for k in range(KD)]
    r_a = [Res(f"a{k}") for k in range(KA)]
    r_g = r_a[0:KD]; r_sq = r_a[KD:2 * KD]; r_q = r_a[0:KD]; r_at = r_sq
    gT = aT[:, 0:KD, :]; sqT = aT[:, KD:2 * KD, :]; qT = aT[:, 0:KD, :]; attT = sqT
    r_w = [Res(f"w{i}") for i in range(cfg.NSLOT)]
    r_bank = [Res(f"bank{i}") for i in range(8)]
    r_tmp = [Res(f"tmp{i}") for i in range(NTMP)]
    r_rstd = Res("rstd"); r_rstd2 = Res("rstd2"); r_rope = Res("rope"); r_kTa = [Res(f"kTa{g}") for g in range(NG)]
    r_vTa = [Res(f"vTa{g}") for g in range(NG)]; r_Vt = [Res(f"Vt{i}") for i in range(NV)]; r_ckT = Res("ckT"); r_cvS = Res("cvS")
    r_E = [Res(f"E{i}") for i in range(NE)]; r_Rr = Res("Rr"); r_eskb = Res("eskb")
    r_stg = [Res("stg0"), Res("stg1")]; r_const = Res("const"); r_zprev = Res("zprev"); r_zo = Res("zo")
    r_rcol = Res("rcol"); r_kst = Res("kst"); r_vst = Res("vst"); r_ckf = r_stg[0]; r_out = Res("out")
    st = {"bank": 0, "tmp": 0, "E": 0, "stg": 0}

    def next_dt():
        if st["bank"] % 2:
            st["bank"] += 1
        d = (st["bank"] // 2) % 4
        st["bank"] = (st["bank"] + 2) % 8
        return pst[d], [r_bank[2 * d], r_bank[2 * d + 1]]

    def next_bank():
        b = st["bank"] % 8
        st["bank"] = (st["bank"] + 1) % 8
        return pst[b // 2][:, (b % 2) * 512:(b % 2) * 512 + 512], r_bank[b]

    tmp_free = list(range(NTMP))
    tmp_idx = {}
    def next_tmp():
        i = tmp_free.pop(0)
        tmp_idx[id(tmp[i])] = i
        return tmp[i], r_tmp[i]
    def free_tmp(*tiles):
        for t in tiles:
            i = tmp_idx.pop(id(t))
            tmp_free.append(i)

    ident = cm[:, 0, :]; perm = cm[:, 3, :]; permb = cmb[:, 3, :]
    onesm = cmb[:, 1, :]; blk1 = cmb[:, 2, :]; ones1 = cmb[:, 4, :]

    ws = {"next_dma": 0, "next_use": 0}
    total_units = cfg.NU * NP
    dw = [dsem(f"w{i}") for i in range(cfg.NSLOT)]

    def w_issue():
        u = ws["next_dma"]
        if u >= total_units:
            return
        s = u % cfg.NSLOT
        dma(POOL, dw[s], wsl[s][:].rearrange("p k c -> p (k c)"), wall[u % cfg.NU], [], [r_w[s]])
        ws["next_dma"] += 1

    def w_get():
        u = ws["next_use"]
        ws["next_use"] += 1
        s = u % cfg.NSLOT
        return wsl[s], r_w[s]

    def w_done():
        w_issue()

    dc = dsem("const"); d_ckf = dsem("ckf"); d_cvs = dsem("cvsin"); d_rope = dsem("rope")
    d_stg = [dsem("stg0"), dsem("stg1")]; d_kst = dsem("kst"); d_vst = dsem("vst"); d_zo = dsem("zo"); d_cp = dsem("cp")
    dma(SP, dc, cm[:].rearrange("p a b -> p (a b)"), cmat_d, [], [r_const])
    dma(SP, dc, pcs[:], pcols_d, [], [r_const])
    dma(SP, dc, scv[:].rearrange("p a b c d -> p (a b c d)"), scv_d, [], [r_const])
    op(ACT, lambda h: h.activation(out=cmb[:], in_=cm[:], func=AF.Copy), [r_const], [r_const])
    so = off["sinks"]
    op(ACT, lambda h: h.activation(out=esk[:], in_=pcs[:, so:so + LB * NG * 8], func=AF.Exp), [r_const], [r_const])
    op(DVE, lambda h: h.memset(zprev[:], 0.0), [], [r_zprev])
    op(DVE, lambda h: h.memset(hmb[:], 0.0), [], [r_const])
    op(DVE, lambda h: h.memset(hmb[:, 0, 0:64], 1.0), [], [r_const])
    op(DVE, lambda h: h.memset(hmb[:, 1, 64:128], 1.0), [], [r_const])
    for g in range(NG):
        op(DVE, lambda h, g=g: h.memset(kTa[:, g, :], 0.0), [], [r_kTa[g]])
        op(DVE, lambda h, g=g: h.memset(vTa[:, g, :], 0.0), [], [r_vTa[g]])
    op(DVE, lambda h: h.memset(epst[:], cfg.EPS), [], [r_const])
    for i in range(NTMP):
        op(DVE, lambda h, i=i: h.memset(tmp[i][:], 0.0), [], [r_tmp[i]])
    for s in range(NSEQ):
        dma(SP, d_stg[0], ckf, ckd_d[s], [], [r_ckf])
        for g in range(NG):
            pt, rb = next_bank()
            op(PE, lambda h, pt=pt, g=g: h.transpose(out=pt[:, 0:128], in_=ckf[:, g * 128:(g + 1) * 128], identity=ident),
               [r_ckf, r_const], [rb])
            op(ACT, lambda h, pt=pt, s=s, g=g: h.activation(out=ckT[:, s, g, :], in_=pt[:, 0:128], func=AF.Copy), [rb], [r_ckT])
        dma(POOL, d_cvs, cvS[:, s, :], cvd_d[s], [], [r_cvS])
    for _ in range(cfg.NSLOT):
        w_issue()

    def pcol(name, k=0):
        return pcs[:, off[name] + k:off[name] + k + 1]

    def proj(nunits, rhs_fn, rhs_res_fn, c0, c1):
        pt, rbs = next_dt()
        tiles = ctiles(c0, c1)
        nk = nunits * KD
        for uu in range(nunits):
            wt, rw = w_get()
            for kk in range(KD):
                kc = uu * KD + kk
                for ti, (cs, n) in enumerate(tiles):
                    op(PE, lambda h, cs=cs, n=n, kk=kk, kc=kc, wt=wt: h.matmul(
                        pt[:, cs - c0:cs - c0 + n], wt[:, kk, :], rhs_fn(kc)[:, cs:cs + n],
                        start=(kc == 0), stop=(kc == nk - 1)),
                       [rw, rhs_res_fn(kc)], [rbs[ti]])
            w_done()
        return pt, rbs[:len(tiles)]

    def cast_h(gname, c0, c1):
        for kc in range(KD):
            op(ACT, lambda h, kc=kc: h.activation(out=hT[:, kc, c0:c1], in_=xT[:, kc, c0:c1], func=AF.Copy,
                                                  scale=pcol(gname, kc)), [r_x[kc], r_const], [r_h[kc]])

    def norm_pre(gname, c0, c1):
        cast_h(gname, c0, c1)
        for kc in range(KD):
            op(ACT, lambda h, kc=kc: h.activation(out=sqT[:, kc, c0:c1], in_=xT[:, kc, c0:c1], func=AF.Square),
               [r_x[kc]], [r_sq[kc]])

    def norm_post(c0, c1, need2=True):
        pt, rbs = next_dt()
        tiles = ctiles(c0, c1)
        for kc in range(KD):
            for ti, (cs, n) in enumerate(tiles):
                op(PE, lambda h, kc=kc, cs=cs, n=n: h.matmul(pt[:, cs - c0:cs - c0 + n], onesm, sqT[:, kc, cs:cs + n],
                                                          start=(kc == 0), stop=(kc == KD - 1)),
                   [r_const, r_sq[kc]], [rbs[ti]])
        rb = rbs[:len(tiles)]
        op(ACT, lambda h: h.activation(out=rstd[:, c0:c1], in_=pt[:, 0:c1 - c0], func=AF.Ln, bias=epsc, scale=1.0),
           rb + [r_const], [r_rstd])
        op(ACT, lambda h: h.activation(out=rstd[:, c0:c1], in_=rstd[:, c0:c1], func=AF.Exp, scale=-0.5), [r_rstd], [r_rstd])
        if need2:
            op(DVE, lambda h: h.tensor_tensor(out=rstd2[:, c0:c1], in0=rstd[:, c0:c1], in1=rstd[:, c0:c1], op=ALU.mult),
               [r_rstd], [r_rstd2])

    def resid_evac(pt, rb, n, c0, c1):
        op(DVE, lambda h: h.tensor_tensor(out=xT[:, n, c0:c1], in0=xT[:, n, c0:c1], in1=pt[:, 0:c1 - c0], op=ALU.add),
           rb + [r_x[n]], [r_x[n]])

    def mlp(l, c0, c1):
        mark_phase(f'L{l}.mlp')
        norm_pre(f"mlpg{l}", c0, c1)
        W = c1 - c0
        def up_evac(fi, pt, rb):
            t, rt = next_tmp()
            op(DVE, lambda h: h.scalar_tensor_tensor(out=t[:, 0:W], in0=pt[:, 0:W], scalar=0.0,
                                                     in1=rstd[:, c0:c1], op0=ALU.max, op1=ALU.mult),
               rb + [r_rstd], [rt])
            op(ACT, lambda h: h.activation(out=aT[:, fi, c0:c1], in_=t[:, 0:W], func=AF.Square), [rt], [r_a[fi]])
            free_tmp(t)
        for half in range(2):
            pend = []
            for fi in range(KF // 2):
                pt, rb = proj(1, lambda kc: hT[:, kc, :], lambda kc: r_h[kc], c0, c1)
                pend.append((fi, pt, rb))
                if half == 0 and fi == 1:
                    norm_post(c0, c1, need2=False)
                if fi >= 1:
                    up_evac(*pend.pop(0))
            while pend:
                up_evac(*pend.pop(0))
            for n in range(KD):
                pt, rb = proj(cfg.HU, lambda kc: aT[:, kc, :], lambda kc: r_a[kc], c0, c1)
                resid_evac(pt, rb, n, c0, c1)

    def headnorm_pipeline(n_items, gcol, c0, c1, out_fn, out_res_fn, want_f32=False, hook=None, after=None):
        W = c1 - c0
        tl = ctiles(0, W)
        S = {}
        def s1(i):
            pt, rb = proj(1, lambda kc: hT[:, kc, :], lambda kc: r_h[kc], c0, c1)
            if hook is not None and i == min(1, n_items - 1):
                hook()
            qr, r_qr = next_tmp()
            op(DVE, lambda h: h.tensor_tensor(out=qr[:, 0:W], in0=pt[:, 0:W], in1=rstd[:, c0:c1], op=ALU.mult),
               rb + [r_rstd], [r_qr])
            sq, r_s = next_tmp()
            sqb = sq[:].bitcast(BF16)
            op(ACT, lambda h: h.activation(out=sqb[:, 0:W], in_=qr[:, 0:W], func=AF.Square), [r_qr], [r_s])
            S[i] = dict(qr=qr, r_qr=r_qr, sqb=sqb, r_s=r_s, sq=sq)
        def s2(i):
            d = S[i]
            p2, rb2 = next_dt()
            for ti, (cs, n) in enumerate(tl):
                op(PE, lambda h, cs=cs, n=n: h.matmul(p2[:, cs:cs + n], blk1, d["sqb"][:, cs:cs + n], start=True, stop=True),
                   [r_const, d["r_s"]], [rb2[ti]])
            free_tmp(d["sq"])
            rh, r_rh = next_tmp()
            op(ACT, lambda h: h.activation(out=rh[:, 0:W], in_=p2[:, 0:W], func=AF.Ln, bias=epsc, scale=1.0),
               rb2[:len(tl)] + [r_const], [r_rh])
            op(ACT, lambda h: h.activation(out=rh[:, 0:W], in_=rh[:, 0:W], func=AF.Exp, scale=-0.5), [r_rh], [r_rh])
            qn, r_qn = next_tmp()
            op(DVE, lambda h: h.scalar_tensor_tensor(out=qn[:, 0:W], in0=d["qr"][:, 0:W], scalar=gcol, in1=rh[:, 0:W],
                                                     op0=ALU.mult, op1=ALU.mult), [d["r_qr"], r_rh, r_const], [r_qn])
            free_tmp(d["qr"], rh)
            qb, r_qb = next_tmp()
            qbb = qb[:].bitcast(BF16)
            op(ACT, lambda h: h.activation(out=qbb[:, 0:W], in_=qn[:, 0:W], func=AF.Copy), [r_qn], [r_qb])
            d["qn"] = qn; d["r_qn"] = r_qn; d["qb"] = qb; d["qbb"] = qbb; d["r_qb"] = r_qb
        def s3(i):
            d = S.pop(i)
            qn, r_qn = d["qn"], d["r_qn"]
            p3, rb3 = next_dt()
            for ti, (cs, n) in enumerate(tl):
                op(PE, lambda h, cs=cs, n=n: h.matmul(p3[:, cs:cs + n], permb, d["qbb"][:, cs:cs + n], start=True, stop=True),
                   [r_const, d["r_qb"]], [rb3[ti]])
            free_tmp(d["qb"])
            t1, r_t1 = next_tmp()
            op(DVE, lambda h: h.tensor_tensor(out=t1[:, 0:W], in0=qn[:, 0:W], in1=cosT[:, c0:c1], op=ALU.mult),
               [r_qn, r_rope], [r_t1])
            t2, r_t2 = next_tmp()
            op(DVE, lambda h: h.tensor_tensor(out=t2[:, 0:W], in0=p3[:, 0:W], in1=sinT[:, c0:c1], op=ALU.mult),
               rb3[:len(tl)] + [r_rope], [r_t2])
            if want_f32:
                op(DVE, lambda h: h.tensor_tensor(out=t1[:, 0:W], in0=t1[:, 0:W], in1=t2[:, 0:W], op=ALU.add),
                   [r_t1, r_t2], [r_t1])
                op(ACT, lambda h: h.activation(out=out_fn(i), in_=t1[:, 0:W], func=AF.Copy), [r_t1], [out_res_fn(i)])
                if after is not None:
                    after(i, t1, r_t1)
                free_tmp(qn, t1, t2)
            else:
                op(DVE, lambda h: h.tensor_tensor(out=out_fn(i), in0=t1[:, 0:W], in1=t2[:, 0:W], op=ALU.add),
                   [r_t1, r_t2], [out_res_fn(i)])
                if after is not None:
                    after(i, None, None)
                free_tmp(qn, t1, t2)
        for step in range(n_items + 2):
            if step < n_items:
                s1(step)
            if 0 <= step - 1 < n_items:
                s2(step - 1)
            if 0 <= step - 2 < n_items:
                s3(step - 2)
            yield step

    def attn_A(lb, g, q0, nq, kblocks):
        NQ = 4 * nq
        rq = [r_q[4 * g], r_q[4 * g + 1], r_q[4 * g + 2], r_q[4 * g + 3]]
        NB = len(kblocks)
        assert NB * NQ <= 1024
        per_bank = 512 // NQ if NB * NQ > 512 else NB
        tiles_ = []
        nt = (NB + per_bank - 1) // per_bank
        for _ in range(nt):
            tiles_.append(next_dt())
        for bi, (kap, rk, vap, rv, nk, bias) in enumerate(kblocks):
            for half in range(2):
                rows = slice(half * 64, half * 64 + 64)
                ps_, rbk2 = tiles_[bi // per_bank]
                co = half * 512 + (bi % per_bank) * NQ
                op(PE, lambda h, ps_=ps_, co=co, rows=rows, kap=kap, nk=nk: h.matmul(
                    ps_[0:nk, co:co + NQ], kap[rows, :], qT[rows, 4 * g:4 * g + 4, q0:q0 + nq],
                    start=True, stop=True), [rk] + rq, [rbk2[half]])
        Es = []
        for bi, (kap, rk, vap, rv, nk, bias) in enumerate(kblocks):
            ps_, rbk2 = tiles_[bi // per_bank]
            co = (bi % per_bank) * NQ
            ei = st["E"] % NE
            st["E"] += 1
            E, rE = Eb[ei], r_E[ei]
            src = ps_[0:nk, :].rearrange("p (a b) -> p a b", b=512)[:, :, co:co + NQ]
            dst = E[0:nk, 0:2 * NQ].rearrange("p (a b) -> p a b", b=NQ)
            if bias is None:
                op(ACT, lambda h, dst=dst, src=src: h.activation(out=dst, in_=src, func=AF.Exp, scale=0.125), rbk2, [rE])
            else:
                op(ACT, lambda h, dst=dst, src=src, bias=bias: h.activation(out=dst, in_=src, func=AF.Exp, scale=0.125,
                                                                            bias=bias), rbk2 + [r_const], [rE])
            Es.append((E, rE, vap, rv, nk))
        return (lb, g, q0, nq, Es)

    def attn_B(state):
        lb, g, q0, nq, Es = state
        NQ = 4 * nq
        pd, rbd = next_bank()
        eo = g * 8
        first = True
        for half in range(2):
            for bi, (E, rE, vap, rv, nk) in enumerate(Es):
                op(PE, lambda h, E=E, nk=nk, half=half, first=first: h.matmul(
                    pd[:, 0:NQ], hmb[0:nk, half, :], E[0:nk, half * NQ:(half + 1) * NQ], start=first, stop=False),
                   [rE, r_const], [rbd])
                first = False
            op(PE, lambda h, half=half: h.matmul(
                pd[:, 0:NQ], hmb[0:1, half, :],
                eskb[0:1, eo * 64:(eo + 8) * 64].rearrange("p (a q) -> p a q", q=64)[:, half * 4:(half + 1) * 4, 0:nq],
                start=False, stop=(half == 1)), [r_const, r_eskb], [rbd])
        op(DVE, lambda h: h.reciprocal(out=Rr[:, 0:NQ], in_=pd[:, 0:NQ]), [rbd], [r_Rr])
        po, rbo = next_bank()
        for bi, (E, rE, vap, rv, nk) in enumerate(Es):
            op(PE, lambda h, E=E, vap=vap, nk=nk, bi=bi: h.matmul(
                po[:, 0:2 * NQ], vap, E[0:nk, 0:2 * NQ],
                start=(bi == 0), stop=(bi == len(Es) - 1)), [rE, rv], [rbo])
        for half in range(2):
            rows = slice(half * 64, half * 64 + 64)
            op(DVE, lambda h, half=half, rows=rows: h.tensor_tensor(
                out=attT[rows, 4 * g:4 * g + 4, q0:q0 + nq],
                in0=po[rows, half * NQ:(half + 1) * NQ].rearrange("p (a q) -> p a q", q=nq),
                in1=Rr[rows, 0:NQ].rearrange("p (a q) -> p a q", q=nq), op=ALU.mult),
               [rbo, r_Rr], [r_at[4 * g], r_at[4 * g + 1], r_at[4 * g + 2], r_at[4 * g + 3]])

    def attn_pipeline(units):
        prev = None
        for u in units:
            cur = attn_A(*u)
            if prev is not None:
                attn_B(prev)
            prev = cur
            yield
        if prev is not None:
            attn_B(prev)
        yield

    marks = []
    def mark_phase(name):
        marks.append((name, PE.cnt, ACT.cnt, DVE.cnt))
    class _Stop(Exception):
        pass
    def stop_at(name):
        if getattr(cfg, "stop", None) == name:
            raise _Stop()
    try:
      stop_at("prologue")
      for p in range(NP):
          T = cfg.T[p]; pc0 = cfg.pc0[p]; npr = cfg.pass_prompt[p]; last = (p == NP - 1)
          pend = pc0 + npr
          b0 = pc0
          nch = npr // 64
          mark_phase(f'p{p}.load')
          dma(SP, d_rope, cosT[:, 0:T], rope_d[p, 0, :, 0:T], [], [r_rope])
          dma(SP, d_rope, sinT[:, 0:T], rope_d[p, 1, :, 0:T], [], [r_rope])
          for (c0, n) in [(c, min(128, T - c)) for c in range(0, T, 128)]:
              for pi in range(KD // KH):
                  si = st["stg"] % 2
                  st["stg"] += 1
                  dma(SP, d_stg[si], stg[si][0:n, :], xin[cfg.R0[p] + c0:cfg.R0[p] + c0 + n, pi * KH * 128:(pi + 1) * KH * 128],
                      [], [r_stg[si]])
                  pt, rbs = next_dt()
                  for i in range(KH):
                      op(PE, lambda h, i=i, si=si, pt=pt, n=n: h.transpose(out=pt[:, i * 128:i * 128 + n],
                                                                          in_=stg[si][0:n, i * 128:(i + 1) * 128],
                                                                          identity=ident[0:n, 0:n]),
                         [r_stg[si], r_const], [rbs[(i * 128) // 512]])
                  op(ACT, lambda h, pt=pt, pi=pi, c0=c0, n=n: h.activation(
                      out=xT[:, pi * KH:(pi + 1) * KH, c0:c0 + n],
                      in_=pt[:, 0:KH * 128].rearrange("p (a b) -> p a b", b=128)[:, :, 0:n], func=AF.Copy),
                     rbs, [r_x[k] for k in range(pi * KH, (pi + 1) * KH)])
          stop_at('load')
          for l in range(LA):
              mark_phase(f'p{p}.A{l}.win')
              norm_pre(f"mixg{l}", 0, T)
              for j in range(KD):
                  pB, rbB = proj(1, lambda kc: hT[:, kc, :], lambda kc: r_h[kc], 0, T)
                  Bs, r_Bs = next_tmp()
                  op(ACT, lambda h: h.activation(out=Bs[:, 0:T], in_=pB[:, 0:T], func=AF.Copy), rbB, [r_Bs])
                  pC, rbC = proj(1, lambda kc: hT[:, kc, :], lambda kc: r_h[kc], 0, T)
                  Cs, r_Cs = next_tmp()
                  op(ACT, lambda h: h.activation(out=Cs[:, 0:T], in_=pC[:, 0:T], func=AF.Copy), rbC, [r_Cs])
                  if j == 0:
                      norm_post(0, T, need2=True)
                  pU, rbU = proj(1, lambda kc: hT[:, kc, :], lambda kc: r_h[kc], 0, T)
                  zb, r_zb = next_tmp()
                  op(DVE, lambda h: h.tensor_tensor(out=zb[:, 2:2 + T], in0=pU[:, 0:T], in1=Cs[:, 0:T], op=ALU.mult),
                     rbU + [r_Cs], [r_zb])
                  op(DVE, lambda h: h.tensor_tensor(out=zb[:, 2:2 + T], in0=zb[:, 2:2 + T], in1=rstd2[:, 0:T], op=ALU.mult),
                     [r_rstd2], [r_zb])
                  gs = 0
                  op(ACT, lambda h, gs=gs: h.activation(out=zb[:, 2 + gs:2 + gs + 2], in_=zprev[:, l, j, :], func=AF.Copy),
                     [r_zprev], [r_zb])
                  if last:
                      s0 = cfg.sc0[p][0] - 2
                      op(ACT, lambda h, s0=s0: h.activation(
                          out=zb[:, 2 + s0:2 + s0 + NSEQ * (SL + 2)].rearrange("p (s t) -> p s t", t=SL + 2)[:, :, 0:2],
                          in_=scv[:, l, j, :, :], func=AF.Copy), [r_const], [r_zb])
                  cvt, r_cv = next_tmp()
                  op(DVE, lambda h: h.tensor_scalar(out=cvt[:, 0:T], in0=zb[:, 2:2 + T], scalar1=pcol(f"cw{l}_2", j),
                                                    scalar2=None, op0=ALU.mult), [r_zb, r_const], [r_cv])
                  op(DVE, lambda h: h.scalar_tensor_tensor(out=cvt[:, 0:T], in0=zb[:, 1:1 + T], scalar=pcol(f"cw{l}_1", j),
                                                           in1=cvt[:, 0:T], op0=ALU.mult, op1=ALU.add), [r_zb, r_const], [r_cv])
                  op(DVE, lambda h: h.scalar_tensor_tensor(out=cvt[:, 0:T], in0=zb[:, 0:T], scalar=pcol(f"cw{l}_0", j),
                                                           in1=cvt[:, 0:T], op0=ALU.mult, op1=ALU.add), [r_zb, r_const], [r_cv])
                  op(ACT, lambda h: h.activation(out=zprev[:, l, j, :], in_=zb[:, 2 + pend - 2:2 + pend], func=AF.Copy),
                     [r_zb], [r_zprev])
                  if last:
                      op(ACT, lambda h: h.activation(out=zo[:, l, j, 0, :], in_=zb[:, 2 + pend - 2:2 + pend], func=AF.Copy),
                         [r_zb], [r_zo])
                      s0 = cfg.sc0[p][0] - 2
                      op(ACT, lambda h, s0=s0: h.activation(
                          out=zo[:, l, j, 1:1 + NSEQ, :],
                          in_=zb[:, 2 + s0:2 + s0 + NSEQ * (SL + 2)].rearrange("p (s t) -> p s t", t=SL + 2)[:, :, SL:SL + 2],
                          func=AF.Copy), [r_zb], [r_zo])
                  op(DVE, lambda h: h.tensor_tensor(out=cvt[:, 0:T], in0=cvt[:, 0:T], in1=Bs[:, 0:T], op=ALU.mult),
                     [r_Bs], [r_cv])
                  op(DVE, lambda h: h.tensor_tensor(out=gT[:, j, 0:T], in0=cvt[:, 0:T], in1=rstd[:, 0:T], op=ALU.mult),
                     [r_cv, r_rstd], [r_g[j]])
                  free_tmp(Bs, Cs, zb, cvt)
              mark_phase(f'p{p}.A{l}.wout')
              for n in range(KD):
                  pt, rb = proj(1, lambda kc: gT[:, kc, :], lambda kc: r_g[kc], 0, T)
                  resid_evac(pt, rb, n, 0, T)
              mlp(l, 0, T)
          if last:
              with nc.allow_non_contiguous_dma(reason="conv state output, tiny"):
                  for l in range(LA):
                      for s in range(1 + NSEQ):
                          for t in range(2):
                              dma(SP, d_zo, zout_d[l, s, t, :].rearrange("(k p) -> p k", p=128), zo[:, l, :, s, t],
                                  [r_zo], [r_out])
          stop_at('A')
          k0 = 6 if p == 0 else pc0
          mark_phase(f'p{p}.kv')
          koff = 128 - pc0
          if p > 0:
              src0 = (128 - cfg.pc0[p - 1]) + cfg.pc0[p - 1] + cfg.pass_prompt[p - 1] - 128
              for g in range(NG):
                  op(ACT, lambda h, g=g: h.activation(out=kTa[:, g, 0:128], in_=kTa[:, g, src0:src0 + 128], func=AF.Copy),
                     [r_kTa[g]], [r_kTa[g]])
                  op(ACT, lambda h, g=g: h.activation(out=vTa[:, g, 0:128], in_=vTa[:, g, src0:src0 + 128], func=AF.Copy),
                     [r_vTa[g]], [r_vTa[g]])
          norm_pre("kvg", k0, T)
          def k_after(g, kf, r_kf):
              if not last:
                  return
              ptk, rbk = next_bank()
              op(PE, lambda h: h.transpose(out=ptk[:, 0:128], in_=kf[:, pend - 128 - k0:pend - k0], identity=ident),
                 [r_kf, r_const], [rbk])
              op(ACT, lambda h: h.activation(out=kst[:, g, :], in_=ptk[:, 0:64], func=AF.Copy), [rbk], [r_kst])
              dma(SP, d_kst, ckp_d[:, g * 64:(g + 1) * 64], kst[:, g, :], [r_kst], [r_out])
              for s in range(NSEQ):
                  c0s = cfg.sc0[p][s]
                  ptk2, rbk2 = next_bank()
                  op(PE, lambda h, ptk2=ptk2, c0s=c0s: h.transpose(out=ptk2[0:SL, 0:128], in_=kf[:, c0s - k0:c0s - k0 + SL],
                                                                   identity=ident), [r_kf, r_const], [rbk2])
                  op(ACT, lambda h, ptk2=ptk2: h.activation(out=kst[0:SL, g, :], in_=ptk2[0:SL, 0:64], func=AF.Copy),
                     [rbk2], [r_kst])
                  dma(SP, d_kst, cks_d[s, 128 - SL:128, g * 64:(g + 1) * 64], kst[0:SL, g, :], [r_kst], [r_out])
          for _ in headnorm_pipeline(NG, pcol("gk"), k0, T, lambda g: kTa[:, g, koff + k0:koff + T], lambda g: r_kTa[g],
                                     want_f32=True, hook=lambda: norm_post(k0, T, need2=False), after=k_after):
              pass
          WV = T - k0
          for g in range(NG):
              pt, rb = proj(1, lambda kc: hT[:, kc, :], lambda kc: r_h[kc], k0, T)
              if last:
                  vT, r_vT = next_tmp()
                  op(DVE, lambda h, pt=pt, vT=vT: h.tensor_tensor(out=vT[:, 0:WV], in0=pt[:, 0:WV], in1=rstd[:, k0:T], op=ALU.mult),
                     rb + [r_rstd], [r_vT])
                  op(ACT, lambda h, g=g, vT=vT: h.activation(out=vTa[:, g, koff + k0:koff + T], in_=vT[:, 0:WV], func=AF.Copy),
                     [r_vT], [r_vTa[g]])
                  ptv, rbv_ = next_bank()
                  op(PE, lambda h, vT=vT, ptv=ptv: h.transpose(out=ptv[:, 0:128], in_=vT[:, pend - 128 - k0:pend - k0], identity=ident),
                     [r_vT, r_const], [rbv_])
                  op(ACT, lambda h, ptv=ptv, g=g: h.activation(out=kst[:, g, :], in_=ptv[:, 0:64], func=AF.Copy), [rbv_], [r_kst])
                  dma(SP, d_kst, cvp_d[:, g * 64:(g + 1) * 64], kst[:, g, :], [r_kst], [r_out])
                  for s in range(NSEQ):
                      c0s = cfg.sc0[p][s]
                      ptv2, rbv2 = next_bank()
                      op(PE, lambda h, vT=vT, ptv2=ptv2, c0s=c0s: h.transpose(out=ptv2[0:SL, 0:128], in_=vT[:, c0s - k0:c0s - k0 + SL],
                                                                             identity=ident), [r_vT, r_const], [rbv2])
                      op(ACT, lambda h, ptv2=ptv2, g=g: h.activation(out=kst[0:SL, g, :], in_=ptv2[0:SL, 0:64], func=AF.Copy),
                         [rbv2], [r_kst])
                      dma(SP, d_kst, cvs_d[s, 128 - SL:128, g * 64:(g + 1) * 64], kst[0:SL, g, :], [r_kst], [r_out])
                  free_tmp(vT)
              else:
                  op(DVE, lambda h, pt=pt, g=g: h.tensor_tensor(out=vTa[:, g, koff + k0:koff + T], in0=pt[:, 0:WV],
                                                                in1=rstd[:, k0:T], op=ALU.mult), rb + [r_rstd], [r_vTa[g]])
          identb = cmb[:, 0, :]
          vt_list = [(m + 2, 128 + 64 * m, 128 if m < nch - 1 else 64) for m in range(-2, nch)] + \
                    [(nch + 2 + s, koff + c0s, SL) for s, c0s in enumerate(cfg.sc0[p])]
          for (vi, i0, n) in vt_list:
              pvf, rbv = next_bank()
              pv = pvf.bitcast(BF16)
              for g in range(NG):
                  op(PE, lambda h, g=g, i0=i0, n=n, pv=pv: h.transpose(out=pv[0:n, g * 128:(g + 1) * 128],
                                                                      in_=vTa[:, g, i0:i0 + n], identity=identb),
                     [r_vTa[g], r_const], [rbv])
              op(ACT, lambda h, vi=vi, n=n, pv=pv: h.activation(out=Vt[0:n, vi, :], in_=pv[0:n, 0:NG * 128], func=AF.Copy),
                 [rbv], [r_Vt[vi]])
          stop_at('KV')
          for lb in range(LB):
              l = LA + lb
              mark_phase(f'p{p}.B{lb}.q')
              norm_pre(f"mixg{l}", b0, T)
              qgen = headnorm_pipeline(KD, pcol(f"gq{lb}"), b0, T, lambda n: qT[:, n, b0:T], lambda n: r_q[n],
                                       hook=lambda: norm_post(b0, T, need2=False))
              qstate = {"done": 0}
              def q_advance(upto):
                  while qstate["done"] < min(upto, KD + 2):
                      next(qgen)
                      qstate["done"] += 1
              mark_phase(f'p{p}.B{lb}.attn')
              op(ACT, lambda h, lb=lb: h.activation(
                  out=eskb[0:1, :].rearrange("p (a q) -> p a q", q=64),
                  in_=esk[0:1, lb * NG * 8:(lb + 1) * NG * 8].unsqueeze(2).to_broadcast([1, NG * 8, 64]), func=AF.Copy),
                 [r_const], [r_eskb])
              for g in range(NG):
                  units = []
                  koff = 128 - pc0
                  for c in range(nch):
                      biasA = None
                      if p == 0 and c < 2:
                          biasA = pcs[:, off["kmask"] + c:off["kmask"] + c + 1]
                      kb = [(kTa[:, g, 64 * c:64 * c + 128], r_kTa[g], Vt[:, c, g * 128:(g + 1) * 128], r_Vt[c], 128, biasA),
                            (kTa[:, g, 128 + 64 * c:192 + 64 * c], r_kTa[g], Vt[0:64, c + 2, g * 128:(g + 1) * 128], r_Vt[c + 2],
                             64, None)]
                      units.append((lb, g, pc0 + 64 * c, 64, kb))
                  for s, c0s in enumerate(cfg.sc0[p]):
                      kb = [(ckT[:, s, g, :], r_ckT, cvS[:, s, g * 128:(g + 1) * 128], r_cvS, 128, None),
                            (kTa[:, g, koff + c0s:koff + c0s + SL], r_kTa[g], Vt[0:SL, nch + 2 + s, g * 128:(g + 1) * 128],
                             r_Vt[nch + 2 + s], SL, None)]
                      units.append((lb, g, c0s, SL, kb))
                  q_advance(4 * g + 6 if INTERLEAVE_QA else KD + 2)
                  for ui, _ in enumerate(attn_pipeline(units)):
                      if INTERLEAVE_QA and ui % 3 == 2:
                          q_advance(qstate["done"] + 1)
              q_advance(KD + 2)
              stop_at('ATT')
              if last:
                  pass
              mark_phase(f'p{p}.B{lb}.wo')
              for n in range(KD):
                  pt, rb = proj(1, lambda kc: attT[:, kc, :], lambda kc: r_at[kc], b0, T)
                  resid_evac(pt, rb, n, b0, T)
              mlp(l, b0, T)
          stop_at('B')
          mark_phase(f'p{p}.store')
          for (c0, n) in [(c, min(128, T - c)) for c in range(b0, T, 128)]:
              for pi in range(KD // KH):
                  pt, rbs = next_dt()
                  for i in range(KH):
                      kc = pi * KH + i
                      op(PE, lambda h, i=i, kc=kc, pt=pt, c0=c0, n=n: h.transpose(out=pt[0:n, i * 128:(i + 1) * 128],
                                                                                 in_=xT[:, kc, c0:c0 + n], identity=ident),
                         [r_x[kc], r_const], [rbs[(i * 128) // 512]])
                  si = st["stg"] % 2
                  st["stg"] += 1
                  op(ACT, lambda h, pt=pt, si=si, n=n: h.activation(out=stg[si][0:n, :], in_=pt[0:n, 0:KH * 128], func=AF.Copy),
                     rbs, [r_stg[si]])
                  ro = cfg.O0[p] + c0 - b0
                  dma(SP, d_stg[si], yout[ro:ro + n, pi * KH * 128:(pi + 1) * KH * 128], stg[si][0:n, :], [r_stg[si]], [r_out])
    except _Stop:
        pass
    for s in range(NSEQ):
        dma(SP, d_cp, cks_d[s, 0:128 - SL, :].rearrange("r (g c) -> r g c", c=64),
            ckd_d[s, SL:128, :].rearrange("r (g c) -> r g c", c=128)[:, :, 0:64], [], [r_out])
        dma(SP, d_cp, cvs_d[s, 0:128 - SL, :].rearrange("r (g c) -> r g c", c=64),
            cvd_d[s, SL:128, :].rearrange("r (g c) -> r g c", c=128)[:, :, 0:64], [], [r_out])
    for ds in (d_stg[0], d_stg[1], d_kst, d_vst, d_zo, d_cp):
        if ds.cnt:
            nc.sync.wait_ge(ds.sem, ds.cnt)
    assert getattr(cfg, 'stop', None) or ws["next_use"] == total_units, (ws["next_use"], total_units)
    mark_phase('end')
    cfg.marks = marks
    es.close()
    return nc


def assemble(cfg, results):
    D, NG, NSEQ, SL = cfg.D, cfg.NG, cfg.NSEQ, cfg.SL
    nco = cfg.NCORES
    yp = np.zeros((1, nco * cfg.PC, D), np.float32)
    ys = np.zeros((nco * NSEQ, SL, D), np.float32)
    scs = np.zeros((cfg.LA, nco * NSEQ, 2, D), np.float32)
    cks = np.zeros((nco * NSEQ, 128, NG, 64), np.float32)
    cvs = np.zeros((nco * NSEQ, 128, NG, 64), np.float32)
    for c in range(nco):
        r = results[c]
        tok = 0
        for p in range(cfg.NP):
            n = cfg.pass_prompt[p]
            yp[0, c * cfg.PC + tok:c * cfg.PC + tok + n] = r["yout"][cfg.O0[p]:cfg.O0[p] + n]
            tok += n
            for s, c0 in enumerate(cfg.sc0[p]):
                o = cfg.O0[p] + c0 - cfg.pc0[p]
                ys[c * NSEQ + s] = r["yout"][o:o + SL]
        scs[:, c * NSEQ:(c + 1) * NSEQ] = r["zout"][:, 1:]
        cks[c * NSEQ:(c + 1) * NSEQ] = r["cks"].reshape(NSEQ, 128, NG, 64)
        cvs[c * NSEQ:(c + 1) * NSEQ] = r["cvs"].reshape(NSEQ, 128, NG, 64)
    rl = results[nco - 1]
    scp = np.ascontiguousarray(rl["zout"][:, 0:1])
    ckp = rl["ckp"].reshape(1, 128, NG, 64)
    cvp = rl["cvp"].reshape(1, 128, NG, 64)
    return (yp, ys, scp, scs, ckp, cvp, cks, cvs)


def run(cfg, inp):
    inp = {k: np.asarray(v, dtype=np.float32) for k, v in inp.items()}
    maps = host_prepare(cfg, inp)
    nc = build_program(cfg)
    res = run_bass_kernel_spmd(nc, maps, core_ids=list(range(cfg.NCORES)))
    return assemble(cfg, res.results)


def kernel(**inputs):
    return run(Cfg(), inputs)
```

```python
import numpy as np
import concourse.bass as bass
import concourse.mybir as mybir
from concourse.bass_utils import run_bass_kernel_spmd
from contextlib import ExitStack

F32, BF16 = mybir.dt.float32, mybir.dt.bfloat16
ALU = mybir.AluOpType
AF = mybir.ActivationFunctionType


class Cfg:
    def __init__(self, D=2048, PC=2048, pass_prompt=(448, 576, 576, 448), PAST=2048, NCORES=8, NSEQ=4):
        self.D = D; self.KD = D // 128; self.FF = 4 * D; self.KF = self.FF // 128
        self.NH = D // 64; self.NG = self.NH // 8
        self.PC = PC; self.NSEQ = NSEQ; self.SL = 32; self.HALO = 132; self.PAST = PAST
        self.pass_prompt = list(pass_prompt); self.NP = len(pass_prompt); self.NCORES = NCORES
        self.EPS = 1e-6; self.THETA = 500000.0; self.LA = 2; self.LB = 2; self.L = 4
        self.NSLOT = 6
        assert sum(pass_prompt) == PC and all(n % 64 == 0 for n in pass_prompt)
        self.pc0 = []; self.T = []; self.sc0 = []
        for p, n in enumerate(self.pass_prompt):
            pc0 = 2 + (self.HALO if p == 0 else 0)
            T = pc0 + n
            sc = []
            if p == self.NP - 1:
                for s in range(NSEQ):
                    sc.append(T + 2); T += 2 + self.SL
            self.pc0.append(pc0); self.T.append(T); self.sc0.append(sc)
        self.TMAX = max(self.T)
        self.R0 = [int(v) for v in np.cumsum([0] + self.T)]
        self.O0 = [int(v) for v in np.cumsum([0] + [t - c for t, c in zip(self.T, self.pc0)])]
        KD, KF = self.KD, self.KF
        self.HU = KF // 2 // KD
        self.NU = self.LA * (3 * KD + KD + 2 * (KF // 2 + KD * self.HU)) + 2 * self.NG + \
            self.LB * (2 * KD + 2 * (KF // 2 + KD * self.HU))
        o = 0; self.off = {}
        def add(name, n):
            nonlocal o
            self.off[name] = o; o += n
        for l in range(self.L):
            add(f"mixg{l}", KD); add(f"mlpg{l}", KD)
        add("kvg", KD)
        for l in range(self.LA):
            for t in range(3):
                add(f"cw{l}_{t}", KD)
        for lb in range(self.LB):
            add(f"gq{lb}", 1)
        add("gk", 1)
        add("sinks", self.LB * self.NG * 8)
        add("kmask", 2)
        self.NPC = o


def _unit(W, k0, n0, KD, dup64=False):
    if dup64:
        blk = W[k0:k0 + KD * 128, n0:n0 + 64]
        blk = np.concatenate([blk, blk], axis=1)
    else:
        blk = W[k0:k0 + KD * 128, n0:n0 + 128]
    return blk.reshape(KD, 128, 128).transpose(1, 0, 2).reshape(128, KD * 128)


def build_wall(cfg, inp):
    KD, KF, D, NG = cfg.KD, cfg.KF, cfg.D, cfg.NG
    wall = np.empty((cfg.NU, 128, KD * 128), np.float32)
    u = 0
    def put(a):
        nonlocal u
        wall[u] = a; u += 1
    def mlp(l):
        for half in range(2):
            for f in range(half * KF // 2, (half + 1) * KF // 2):
                put(_unit(inp["w_up"][l], 0, f * 128, KD))
            for n in range(KD):
                for uu in range(cfg.HU):
                    put(_unit(inp["w_down"][l], half * (cfg.FF // 2) + uu * KD * 128, n * 128, KD))
    for l in range(cfg.LA):
        for j in range(KD):
            for part in range(3):
                put(_unit(inp["conv_w_in"][l], 0, part * D + j * 128, KD))
        for n in range(KD):
            put(_unit(inp["conv_w_out"][l], 0, n * 128, KD))
        mlp(l)
    for g in range(NG):
        put(_unit(inp["w_kv"], 0, g * 64, KD, dup64=True))
    for g in range(NG):
        put(_unit(inp["w_kv"], 0, NG * 64 + g * 64, KD, dup64=True))
    for lb in range(cfg.LB):
        for n in range(KD):
            put(_unit(inp["w_q"][lb], 0, n * 128, KD))
        for n in range(KD):
            put(_unit(inp["w_o"][lb], 0, n * 128, KD))
        mlp(cfg.LA + lb)
    assert u == cfg.NU
    return wall


def _fm(vec, KD):
    return np.ascontiguousarray(vec.reshape(KD, 128).T)


def host_prepare(cfg, inp):
    KD, D, NG, NSEQ = cfg.KD, cfg.D, cfg.NG, cfg.NSEQ
    wall = build_wall(cfg, inp)
    cm = np.zeros((128, 5, 128), np.float32)
    cm[:, 0, :] = np.eye(128)
    cm[:, 1, :] = 1.0 / D
    cm[0:64, 2, 0:64] = 1.0 / 64; cm[64:128, 2, 64:128] = 1.0 / 64
    for hb in (0, 64):
        for d in range(8):
            cm[hb + d + 8, 3, hb + d] = -1.0
            cm[hb + d, 3, hb + d + 8] = 1.0
    cm[:, 4, :] = 1.0
    cm = cm.reshape(128, 5 * 128)
    half = 8
    inv = (np.float32(cfg.THETA) ** (-np.arange(half, dtype=np.float32) / np.float32(half))).astype(np.float32)
    maps = []
    xp = inp["x_prompt"][0]
    for c in range(cfg.NCORES):
        base = c * cfg.PC
        xin = np.zeros((cfg.R0[-1], D), np.float32)
        pos = np.zeros((cfg.NP, cfg.TMAX), np.float32)
        tok = 0
        for p in range(cfg.NP):
            r0 = cfg.R0[p]
            if p == 0:
                lo = base - cfg.HALO
                if lo >= 0:
                    xin[r0 + 2:r0 + 2 + cfg.HALO] = xp[lo:base]
                pos[p, 2:2 + cfg.HALO] = np.arange(lo, base)
            n = cfg.pass_prompt[p]
            xin[r0 + cfg.pc0[p]:r0 + cfg.pc0[p] + n] = xp[base + tok:base + tok + n]
            pos[p, cfg.pc0[p]:cfg.pc0[p] + n] = np.arange(base + tok, base + tok + n)
            tok += n
            for s, c0 in enumerate(cfg.sc0[p]):
                xin[r0 + c0:r0 + c0 + cfg.SL] = inp["x_sample"][c * NSEQ + s]
                pos[p, c0:c0 + cfg.SL] = cfg.PAST + np.arange(cfg.SL)
        pos = pos.astype(np.float32)
        ang = pos[:, None, :] * inv[None, :, None]
        cosv, sinv = np.cos(ang).astype(np.float32), np.sin(ang).astype(np.float32)
        rope = np.zeros((cfg.NP, 2, 128, cfg.TMAX), np.float32)
        rope[:, 0] = 1.0
        for hb in (0, 64):
            rope[:, 0, hb:hb + 8] = cosv; rope[:, 0, hb + 8:hb + 16] = cosv
            rope[:, 1, hb:hb + 8] = sinv; rope[:, 1, hb + 8:hb + 16] = sinv
        pc = np.zeros((128, cfg.NPC), np.float32)
        o = cfg.off
        for l in range(cfg.L):
            pc[:, o[f"mixg{l}"]:o[f"mixg{l}"] + KD] = _fm(inp["mix_norm_g"][l], KD)
            pc[:, o[f"mlpg{l}"]:o[f"mlpg{l}"] + KD] = _fm(inp["mlp_norm_g"][l], KD)
        pc[:, o["kvg"]:o["kvg"] + KD] = _fm(inp["kv_norm_g"], KD)
        for l in range(cfg.LA):
            for t in range(3):
                pc[:, o[f"cw{l}_{t}"]:o[f"cw{l}_{t}"] + KD] = _fm(inp["conv_w"][l, t], KD)
        for lb in range(cfg.LB):
            pc[:, o[f"gq{lb}"]] = np.tile(inp["q_norm_g"][lb], 2)
        pc[:, o["gk"]] = np.tile(inp["k_norm_g"], 2)
        sk = inp["sinks"].reshape(cfg.LB, NG, 4, 2).transpose(0, 1, 3, 2).reshape(-1)
        pc[:, o["sinks"]:o["sinks"] + sk.size] = sk[None, :]
        if c == 0:
            pc[:, o["kmask"]] = -30000.0
            pc[0:64, o["kmask"] + 1] = -30000.0
        sc = inp["state_conv"][:, c * NSEQ:(c + 1) * NSEQ]
        scv = sc.reshape(cfg.LA, NSEQ, 2, KD, 128).transpose(4, 0, 3, 1, 2).reshape(128, -1)
        ck = inp["cache_k"][c * NSEQ:(c + 1) * NSEQ]
        cv = inp["cache_v"][c * NSEQ:(c + 1) * NSEQ]
        ckd = np.stack([ck, ck], axis=3).reshape(NSEQ, 128, NG * 128)
        cvd = np.stack([cv, cv], axis=3).reshape(NSEQ, 128, NG * 128)
        maps.append({"xin": xin, "wall": wall, "pcols": pc, "scv": np.ascontiguousarray(scv), "cmat": cm,
                     "rope": rope, "ckd": np.ascontiguousarray(ckd), "cvd": np.ascontiguousarray(cvd)})
    return maps


INTERLEAVE_QA = False


class Tok:
    __slots__ = ("sem", "key", "val", "eng")
    def __init__(self, sem, key, val, eng):
        self.sem = sem; self.key = key; self.val = val; self.eng = eng


class Res:
    __slots__ = ("w", "r", "name")
    def __init__(self, name=""):
        self.w = None; self.r = {}; self.name = name


class Eng:
    def __init__(self, name, h, sem):
        self.name = name; self.h = h; self.sem = sem; self.cnt = 0; self.seen = {}; self.key = "e_" + name


class DSem:
    def __init__(self, name, sem):
        self.sem = sem; self.cnt = 0; self.key = "d_" + name


def ctiles(c0, c1):
    out = []
    c = c0
    while c < c1:
        n = min(512, c1 - c)
        out.append((c, n)); c += n
    return out


def build_program(cfg):
    nc = bass.Bass("TRN2", target_bir_lowering=False)
    D, KD, KF, NG, NSEQ, SL, TMAX, NP = cfg.D, cfg.KD, cfg.KF, cfg.NG, cfg.NSEQ, cfg.SL, cfg.TMAX, cfg.NP
    KA = max(KF // 2, 2 * KD)
    LA, LB = cfg.LA, cfg.LB
    off = cfg.off
    NV = 2 + max(cfg.pass_prompt) // 64 + NSEQ
    es = ExitStack()
    def dram(name, shape, kind):
        return nc.dram_tensor(name, list(shape), F32, kind=kind).ap()
    xin = dram("xin", [cfg.R0[-1], D], "ExternalInput")
    wall = dram("wall", [cfg.NU, 128, KD * 128], "ExternalInput")
    pcols_d = dram("pcols", [128, cfg.NPC], "ExternalInput")
    scv_d = dram("scv", [128, LA * KD * NSEQ * 2], "ExternalInput")
    cmat_d = dram("cmat", [128, 5 * 128], "ExternalInput")
    rope_d = dram("rope", [NP, 2, 128, TMAX], "ExternalInput")
    ckd_d = dram("ckd", [NSEQ, 128, NG * 128], "ExternalInput")
    cvd_d = dram("cvd", [NSEQ, 128, NG * 128], "ExternalInput")
    yout = dram("yout", [cfg.O0[-1], D], "ExternalOutput")
    zout_d = dram("zout", [LA, 1 + NSEQ, 2, D], "ExternalOutput")
    ckp_d = dram("ckp", [128, NG * 64], "ExternalOutput")
    cvp_d = dram("cvp", [128, NG * 64], "ExternalOutput")
    cks_d = dram("cks", [NSEQ, 128, NG * 64], "ExternalOutput")
    cvs_d = dram("cvs", [NSEQ, 128, NG * 64], "ExternalOutput")

    def sb(name, shape, dt=F32):
        return es.enter_context(nc.sbuf_tensor("sb_" + name, list(shape), dt))
    xT = sb("xT", [128, KD, TMAX]); hT = sb("hT", [128, KD, TMAX], BF16); aT = sb("aT", [128, KA, TMAX], BF16)
    wsl = [sb(f"w{i}", [128, KD, 128], BF16) for i in range(cfg.NSLOT)]
    NTMP = 7
    tmp = [sb(f"tmp{i}", [128, TMAX + 4]) for i in range(NTMP)]
    rstd = sb("rstd", [128, TMAX]); rstd2 = sb("rstd2", [128, TMAX])
    cosT = sb("cosT", [128, TMAX]); sinT = sb("sinT", [128, TMAX])
    KW = 128 + TMAX
    kTa = sb("kTa", [128, NG, KW], BF16); vTa = sb("vTa", [128, NG, KW], BF16)
    Vt = sb("Vt", [128, NV, NG * 128], BF16)
    ckT = sb("ckT", [128, NSEQ, NG, 128], BF16); cvS = sb("cvS", [128, NSEQ, NG * 128], BF16)
    NE = 4
    Eb = [sb(f"E{i}", [128, 512], BF16) for i in range(NE)]
    Rr = sb("Rr", [128, 512]); hmb = sb("hmb", [128, 2, 128], BF16)
    KH = min(8, KD)
    stg = [sb(f"stg{i}", [128, KH * 128]) for i in range(2)]
    cm = sb("cm", [128, 5, 128]); cmb = sb("cmb", [128, 5, 128], BF16)
    pcs = sb("pcs", [128, cfg.NPC]); esk = sb("esk", [128, LB * NG * 8]); eskb = sb("eskb", [1, NG * 8 * 64], BF16)
    scv = sb("scv", [128, LA, KD, NSEQ, 2]); zprev = sb("zprev", [128, LA, KD, 2]); zo = sb("zo", [128, LA, KD, 1 + NSEQ, 2])
    epst = sb("epst", [128, 1]); epsc = epst[:, 0:1]
    kst = sb("kst", [128, NG, 64])
    ckf = stg[0][:, 0:NG * 128]
    pst = [es.enter_context(nc.psum_tensor(f"ps{i}", [128, 1024], F32)) for i in range(4)]

    def sem(name):
        return es.enter_context(nc.semaphore(name))
    PE = Eng("pe", nc.tensor, sem("s_pe")); ACT = Eng("act", nc.scalar, sem("s_act"))
    DVE = Eng("dve", nc.vector, sem("s_dve")); POOL = Eng("pool", nc.gpsimd, sem("s_pool")); SP = Eng("sp", nc.sync, sem("s_sp"))
    dsems = {}
    def dsem(name):
        if name not in dsems:
            dsems[name] = DSem(name, sem("d_" + name))
        return dsems[name]

    def sync_for(eng, reads, writes):
        toks = []
        for R in reads:
            if R.w is not None:
                toks.append(R.w)
        for R in writes:
            if R.w is not None:
                toks.append(R.w)
            toks.extend(R.r.values())
        for t in toks:
            if t.eng is eng and eng.name == "pe":
                continue
            if eng.seen.get(t.key, 0) >= t.val:
                continue
            eng.h.wait_ge(t.sem, t.val)
            eng.seen[t.key] = t.val

    def mark(tok, reads, writes):
        for R in reads:
            o = R.r.get(tok.key)
            if o is None or o.val < tok.val:
                R.r[tok.key] = tok
        for R in writes:
            R.w = tok; R.r = {}

    def op(eng, fn, reads, writes):
        sync_for(eng, reads, writes)
        ins = fn(eng.h)
        eng.cnt += 1
        ins.then_inc(eng.sem, 1)
        mark(Tok(eng.sem, eng.key, eng.cnt, eng), reads, writes)

    def dma(eng, ds, out, in_, reads, writes, **kw):
        sync_for(eng, reads, writes)
        ins = eng.h.dma_start(out=out, in_=in_, **kw)
        ds.cnt += 16
        ins.then_inc(ds.sem, 16)
        mark(Tok(ds.sem, ds.key, ds.cnt, None), reads, writes)

    r_x = [Res(f"x{k}") for k in range(KD)]; r_h = [Res(f"h{k}") for k in range(KD)]
    r_a = [Res(f"a{k}") for k in range(KA)]
    r_g = r_a[0:KD]; r_sq = r_a[KD:2 * KD]; r_q = r_a[0:KD]; r_at = r_sq
    gT = aT[:, 0:KD, :]; sqT = aT[:, KD:2 * KD, :]; qT = aT[:, 0:KD, :]; attT = sqT
    r_w = [Res(f"w{i}") for i in range(cfg.NSLOT)]
    r_bank = [Res(f"bank{i}") for i in range(8)]
    r_tmp = [Res(f"tmp{i}") for i in range(NTMP)]
    r_rstd = Res("rstd"); r_rstd2 = Res("rstd2"); r_rope = Res("rope"); r_kTa = [Res(f"kTa{g}") for g in range(NG)]
    r_vTa = [Res(f"vTa{g}") for g in range(NG)]; r_Vt = [Res(f"Vt{i}") for i in range(NV)]; r_ckT = Res("ckT"); r_cvS = Res("cvS")
    r_E = [Res(f"E{i}") for i in range(NE)]; r_Rr = Res("Rr"); r_eskb = Res("eskb")
    r_stg = [Res("stg0"), Res("stg1")]; r_const = Res("const"); r_zprev = Res("zprev"); r_zo = Res("zo")
    r_rcol = Res("rcol"); r_kst = Res("kst"); r_vst = Res("vst"); r_ckf = r_stg[0]; r_out = Res("out")
    st = {"bank": 0, "tmp": 0, "E": 0, "stg": 0}

    def next_dt():
        if st["bank"] % 2:
            st["bank"] += 1
        d = (st["bank"] // 2) % 4
        st["bank"] = (st["bank"] + 2) % 8
        return pst[d], [r_bank[2 * d], r_bank[2 * d + 1]]

    def next_bank():
        b = st["bank"] % 8
        st["bank"] = (st["bank"] + 1) % 8
        return pst[b // 2][:, (b % 2) * 512:(b % 2) * 512 + 512], r_bank[b]

    tmp_free = list(range(NTMP))
    tmp_idx = {}
    def next_tmp():
        i = tmp_free.pop(0)
        tmp_idx[id(tmp[i])] = i
        return tmp[i], r_tmp[i]
    def free_tmp(*tiles):
        for t in tiles:
            i = tmp_idx.pop(id(t))
            tmp_free.append(i)

    ident = cm[:, 0, :]; perm = cm[:, 3, :]; permb = cmb[:, 3, :]
    onesm = cmb[:, 1, :]; blk1 = cmb[:, 2, :]; ones1 = cmb[:, 4, :]

    ws = {"next_dma": 0, "next_use": 0}
    total_units = cfg.NU * NP
    dw = [dsem(f"w{i}") for i in range(cfg.NSLOT)]

    def w_issue():
        u = ws["next_dma"]
        if u >= total_units:
            return
        s = u % cfg.NSLOT
        dma(POOL, dw[s], wsl[s][:].rearrange("p k c -> p (k c)"), wall[u % cfg.NU], [], [r_w[s]])
        ws["next_dma"] += 1

    def w_get():
        u = ws["next_use"]
        ws["next_use"] += 1
        s = u % cfg.NSLOT
        return wsl[s], r_w[s]

    def w_done():
        w_issue()

    dc = dsem("const"); d_ckf = dsem("ckf"); d_cvs = dsem("cvsin"); d_rope = dsem("rope")
    d_stg = [dsem("stg0"), dsem("stg1")]; d_kst = dsem("kst"); d_vst = dsem("vst"); d_zo = dsem("zo"); d_cp = dsem("cp")
    dma(SP, dc, cm[:].rearrange("p a b -> p (a b)"), cmat_d, [], [r_const])
    dma(SP, dc, pcs[:], pcols_d, [], [r_const])
    dma(SP, dc, scv[:].rearrange("p a b c d -> p (a b c d)"), scv_d, [], [r_const])
    op(ACT, lambda h: h.activation(out=cmb[:], in_=cm[:], func=AF.Copy), [r_const], [r_const])
    so = off["sinks"]
    op(ACT, lambda h: h.activation(out=esk[:], in_=pcs[:, so:so + LB * NG * 8], func=AF.Exp), [r_const], [r_const])
    op(DVE, lambda h: h.memset(zprev[:], 0.0), [], [r_zprev])
    op(DVE, lambda h: h.memset(hmb[:], 0.0), [], [r_const])
    op(DVE, lambda h: h.memset(hmb[:, 0, 0:64], 1.0), [], [r_const])
    op(DVE, lambda h: h.memset(hmb[:, 1, 64:128], 1.0), [], [r_const])
    for g in range(NG):
        op(DVE, lambda h, g=g: h.memset(kTa[:, g, :], 0.0), [], [r_kTa[g]])
        op(DVE, lambda h, g=g: h.memset(vTa[:, g, :], 0.0), [], [r_vTa[g]])
    op(DVE, lambda h: h.memset(epst[:], cfg.EPS), [], [r_const])
    for i in range(NTMP):
        op(DVE, lambda h, i=i: h.memset(tmp[i][:], 0.0), [], [r_tmp[i]])
    for s in range(NSEQ):
        dma(SP, d_stg[0], ckf, ckd_d[s], [], [r_ckf])
        for g in range(NG):
            pt, rb = next_bank()
            op(PE, lambda h, pt=pt, g=g: h.transpose(out=pt[:, 0:128], in_=ckf[:, g * 128:(g + 1) * 128], identity=ident),
               [r_ckf, r_const], [rb])
            op(ACT, lambda h, pt=pt, s=s, g=g: h.activation(out=ckT[:, s, g, :], in_=pt[:, 0:128], func=AF.Copy), [rb], [r_ckT])
        dma(POOL, d_cvs, cvS[:, s, :], cvd_d[s], [], [r_cvS])
    for _ in range(cfg.NSLOT):
        w_issue()

    def pcol(name, k=0):
        return pcs[:, off[name] + k:off[name] + k + 1]

    def proj(nunits, rhs_fn, rhs_res_fn, c0, c1):
        pt, rbs = next_dt()
        tiles = ctiles(c0, c1)
        nk = nunits * KD
        for uu in range(nunits):
            wt, rw = w_get()
            for kk in range(KD):
                kc = uu * KD + kk
                for ti, (cs, n) in enumerate(tiles):
                    op(PE, lambda h, cs=cs, n=n, kk=kk, kc=kc, wt=wt: h.matmul(
                        pt[:, cs - c0:cs - c0 + n], wt[:, kk, :], rhs_fn(kc)[:, cs:cs + n],
                        start=(kc == 0), stop=(kc == nk - 1)),
                       [rw, rhs_res_fn(kc)], [rbs[ti]])
            w_done()
        return pt, rbs[:len(tiles)]

    def cast_h(gname, c0, c1):
        for kc in range(KD):
            if kc % 2 == 0:
                op(ACT, lambda h, kc=kc: h.activation(out=hT[:, kc, c0:c1], in_=xT[:, kc, c0:c1], func=AF.Copy,
                                                      scale=pcol(gname, kc)), [r_x[kc], r_const], [r_h[kc]])
            else:
                op(DVE, lambda h, kc=kc: h.tensor_scalar(out=hT[:, kc, c0:c1], in0=xT[:, kc, c0:c1], scalar1=pcol(gname, kc),
                                                         scalar2=None, op0=ALU.mult), [r_x[kc], r_const], [r_h[kc]])

    def norm_pre(gname, c0, c1):
        cast_h(gname, c0, c1)
        for kc in range(KD):
            op(ACT, lambda h, kc=kc: h.activation(out=sqT[:, kc, c0:c1], in_=xT[:, kc, c0:c1], func=AF.Square),
               [r_x[kc]], [r_sq[kc]])

    def norm_post(c0, c1, need2=True):
        pt, rbs = next_dt()
        tiles = ctiles(c0, c1)
        for kc in range(KD):
            for ti, (cs, n) in enumerate(tiles):
                op(PE, lambda h, kc=kc, cs=cs, n=n: h.matmul(pt[:, cs - c0:cs - c0 + n], onesm, sqT[:, kc, cs:cs + n],
                                                          start=(kc == 0), stop=(kc == KD - 1)),
                   [r_const, r_sq[kc]], [rbs[ti]])
        rb = rbs[:len(tiles)]
        op(ACT, lambda h: h.activation(out=rstd[:, c0:c1], in_=pt[:, 0:c1 - c0], func=AF.Ln, bias=epsc, scale=1.0),
           rb + [r_const], [r_rstd])
        op(ACT, lambda h: h.activation(out=rstd[:, c0:c1], in_=rstd[:, c0:c1], func=AF.Exp, scale=-0.5), [r_rstd], [r_rstd])
        if need2:
            op(DVE, lambda h: h.tensor_tensor(out=rstd2[:, c0:c1], in0=rstd[:, c0:c1], in1=rstd[:, c0:c1], op=ALU.mult),
               [r_rstd], [r_rstd2])

    def resid_evac(pt, rb, n, c0, c1):
        op(DVE, lambda h: h.tensor_tensor(out=xT[:, n, c0:c1], in0=xT[:, n, c0:c1], in1=pt[:, 0:c1 - c0], op=ALU.add),
           rb + [r_x[n]], [r_x[n]])

    def mlp(l, c0, c1):
        mark_phase(f'L{l}.mlp')
        norm_pre(f"mlpg{l}", c0, c1)
        W = c1 - c0
        def up_evac(fi, pt, rb):
            t, rt = next_tmp()
            op(DVE, lambda h: h.scalar_tensor_tensor(out=t[:, 0:W], in0=pt[:, 0:W], scalar=0.0,
                                                     in1=rstd[:, c0:c1], op0=ALU.max, op1=ALU.mult),
               rb + [r_rstd], [rt])
            op(ACT, lambda h: h.activation(out=aT[:, fi, c0:c1], in_=t[:, 0:W], func=AF.Square), [rt], [r_a[fi]])
            free_tmp(t)
        for half in range(2):
            pend = []
            for fi in range(KF // 2):
                pt, rb = proj(1, lambda kc: hT[:, kc, :], lambda kc: r_h[kc], c0, c1)
                pend.append((fi, pt, rb))
                if half == 0 and fi == 1:
                    norm_post(c0, c1, need2=False)
                if fi >= 1:
                    up_evac(*pend.pop(0))
            while pend:
                up_evac(*pend.pop(0))
            for n in range(KD):
                pt, rb = proj(cfg.HU, lambda kc: aT[:, kc, :], lambda kc: r_a[kc], c0, c1)
                resid_evac(pt, rb, n, c0, c1)

    def headnorm_pipeline(n_items, gcol, c0, c1, out_fn, out_res_fn, want_f32=False, hook=None, after=None):
        W = c1 - c0
        tl = ctiles(0, W)
        S = {}
        def s1(i):
            pt, rb = proj(1, lambda kc: hT[:, kc, :], lambda kc: r_h[kc], c0, c1)
            if hook is not None and i == min(1, n_items - 1):
                hook()
            qr, r_qr = next_tmp()
            op(DVE, lambda h: h.tensor_tensor(out=qr[:, 0:W], in0=pt[:, 0:W], in1=rstd[:, c0:c1], op=ALU.mult),
               rb + [r_rstd], [r_qr])
            sq, r_s = next_tmp()
            sqb = sq[:].bitcast(BF16)
            op(ACT, lambda h: h.activation(out=sqb[:, 0:W], in_=qr[:, 0:W], func=AF.Square), [r_qr], [r_s])
            S[i] = dict(qr=qr, r_qr=r_qr, sqb=sqb, r_s=r_s, sq=sq)
        def s2(i):
            d = S[i]
            p2, rb2 = next_dt()
            for ti, (cs, n) in enumerate(tl):
                op(PE, lambda h, cs=cs, n=n: h.matmul(p2[:, cs:cs + n], blk1, d["sqb"][:, cs:cs + n], start=True, stop=True),
                   [r_const, d["r_s"]], [rb2[ti]])
            free_tmp(d["sq"])
            rh, r_rh = next_tmp()
            op(ACT, lambda h: h.activation(out=rh[:, 0:W], in_=p2[:, 0:W], func=AF.Ln, bias=epsc, scale=1.0),
               rb2[:len(tl)] + [r_const], [r_rh])
            op(ACT, lambda h: h.activation(out=rh[:, 0:W], in_=rh[:, 0:W], func=AF.Exp, scale=-0.5), [r_rh], [r_rh])
            qn, r_qn = next_tmp()
            op(DVE, lambda h: h.scalar_tensor_tensor(out=qn[:, 0:W], in0=d["qr"][:, 0:W], scalar=gcol, in1=rh[:, 0:W],
                                                     op0=ALU.mult, op1=ALU.mult), [d["r_qr"], r_rh, r_const], [r_qn])
            free_tmp(d["qr"], rh)
            qb, r_qb = next_tmp()
            qbb = qb[:].bitcast(BF16)
            op(ACT, lambda h: h.activation(out=qbb[:, 0:W], in_=qn[:, 0:W], func=AF.Copy), [r_qn], [r_qb])
            d["qn"] = qn; d["r_qn"] = r_qn; d["qb"] = qb; d["qbb"] = qbb; d["r_qb"] = r_qb
        def s3(i):
            d = S.pop(i)
            qn, r_qn = d["qn"], d["r_qn"]
            p3, rb3 = next_dt()
            for ti, (cs, n) in enumerate(tl):
                op(PE, lambda h, cs=cs, n=n: h.matmul(p3[:, cs:cs + n], permb, d["qbb"][:, cs:cs + n], start=True, stop=True),
                   [r_const, d["r_qb"]], [rb3[ti]])
            free_tmp(d["qb"])
            t1, r_t1 = next_tmp()
            op(DVE, lambda h: h.tensor_tensor(out=t1[:, 0:W], in0=qn[:, 0:W], in1=cosT[:, c0:c1], op=ALU.mult),
               [r_qn, r_rope], [r_t1])
            t2, r_t2 = next_tmp()
            op(DVE, lambda h: h.tensor_tensor(out=t2[:, 0:W], in0=p3[:, 0:W], in1=sinT[:, c0:c1], op=ALU.mult),
               rb3[:len(tl)] + [r_rope], [r_t2])
            if want_f32:
                op(DVE, lambda h: h.tensor_tensor(out=t1[:, 0:W], in0=t1[:, 0:W], in1=t2[:, 0:W], op=ALU.add),
                   [r_t1, r_t2], [r_t1])
                op(ACT, lambda h: h.activation(out=out_fn(i), in_=t1[:, 0:W], func=AF.Copy), [r_t1], [out_res_fn(i)])
                if after is not None:
                    after(i, t1, r_t1)
                free_tmp(qn, t1, t2)
            else:
                op(DVE, lambda h: h.tensor_tensor(out=out_fn(i), in0=t1[:, 0:W], in1=t2[:, 0:W], op=ALU.add),
                   [r_t1, r_t2], [out_res_fn(i)])
                if after is not None:
                    after(i, None, None)
                free_tmp(qn, t1, t2)
        for step in range(n_items + 2):
            if step < n_items:
                s1(step)
            if 0 <= step - 1 < n_items:
                s2(step - 1)
            if 0 <= step - 2 < n_items:
                s3(step - 2)
            yield step

    def attn_A(lb, g, q0, nq, kblocks):
        NQ = 4 * nq
        rq = [r_q[4 * g], r_q[4 * g + 1], r_q[4 * g + 2], r_q[4 * g + 3]]
        NB = len(kblocks)
        assert NB * NQ <= 1024
        per_bank = 512 // NQ if NB * NQ > 512 else NB
        tiles_ = []
        nt = (NB + per_bank - 1) // per_bank
        for _ in range(nt):
            tiles_.append(next_dt())
        for bi, (kap, rk, vap, rv, nk, bias) in enumerate(kblocks):
            for half in range(2):
                rows = slice(half * 64, half * 64 + 64)
                ps_, rbk2 = tiles_[bi // per_bank]
                co = half * 512 + (bi % per_bank) * NQ
                op(PE, lambda h, ps_=ps_, co=co, rows=rows, kap=kap, nk=nk: h.matmul(
                    ps_[0:nk, co:co + NQ], kap[rows, :], qT[rows, 4 * g:4 * g + 4, q0:q0 + nq],
                    start=True, stop=True), [rk] + rq, [rbk2[half]])
        Es = []
        for bi, (kap, rk, vap, rv, nk, bias) in enumerate(kblocks):
            ps_, rbk2 = tiles_[bi // per_bank]
            co = (bi % per_bank) * NQ
            ei = st["E"] % NE
            st["E"] += 1
            E, rE = Eb[ei], r_E[ei]
            src = ps_[0:nk, :].rearrange("p (a b) -> p a b", b=512)[:, :, co:co + NQ]
            dst = E[0:nk, 0:2 * NQ].rearrange("p (a b) -> p a b", b=NQ)
            if bias is None:
                op(ACT, lambda h, dst=dst, src=src: h.activation(out=dst, in_=src, func=AF.Exp, scale=0.125), rbk2, [rE])
            else:
                op(ACT, lambda h, dst=dst, src=src, bias=bias: h.activation(out=dst, in_=src, func=AF.Exp, scale=0.125,
                                                                            bias=bias), rbk2 + [r_const], [rE])
            Es.append((E, rE, vap, rv, nk))
        return (lb, g, q0, nq, Es)

    def attn_B(state):
        lb, g, q0, nq, Es = state
        NQ = 4 * nq
        pd, rbd = next_bank()
        eo = g * 8
        first = True
        for half in range(2):
            for bi, (E, rE, vap, rv, nk) in enumerate(Es):
                op(PE, lambda h, E=E, nk=nk, half=half, first=first: h.matmul(
                    pd[:, 0:NQ], hmb[0:nk, half, :], E[0:nk, half * NQ:(half + 1) * NQ], start=first, stop=False),
                   [rE, r_const], [rbd])
                first = False
            op(PE, lambda h, half=half: h.matmul(
                pd[:, 0:NQ], hmb[0:1, half, :],
                eskb[0:1, eo * 64:(eo + 8) * 64].rearrange("p (a q) -> p a q", q=64)[:, half * 4:(half + 1) * 4, 0:nq],
                start=False, stop=(half == 1)), [r_const, r_eskb], [rbd])
        op(DVE, lambda h: h.reciprocal(out=Rr[:, 0:NQ], in_=pd[:, 0:NQ]), [rbd], [r_Rr])
        po, rbo = next_bank()
        for bi, (E, rE, vap, rv, nk) in enumerate(Es):
            op(PE, lambda h, E=E, vap=vap, nk=nk, bi=bi: h.matmul(
                po[:, 0:2 * NQ], vap, E[0:nk, 0:2 * NQ],
                start=(bi == 0), stop=(bi == len(Es) - 1)), [rE, rv], [rbo])
        for half in range(2):
            rows = slice(half * 64, half * 64 + 64)
            op(DVE, lambda h, half=half, rows=rows: h.tensor_tensor(
                out=attT[rows, 4 * g:4 * g + 4, q0:q0 + nq],
                in0=po[rows, half * NQ:(half + 1) * NQ].rearrange("p (a q) -> p a q", q=nq),
                in1=Rr[rows, 0:NQ].rearrange("p (a q) -> p a q", q=nq), op=ALU.mult),
               [rbo, r_Rr], [r_at[4 * g], r_at[4 * g + 1], r_at[4 * g + 2], r_at[4 * g + 3]])

    def attn_pipeline(units):
        prev = None
        for u in units:
            cur = attn_A(*u)
            if prev is not None:
                attn_B(prev)
            prev = cur
            yield
        if prev is not None:
            attn_B(prev)
        yield

    marks = []
    def mark_phase(name):
        marks.append((name, PE.cnt, ACT.cnt, DVE.cnt))
    class _Stop(Exception):
        pass
    def stop_at(name):
        if getattr(cfg, "stop", None) == name:
            raise _Stop()
    try:
      stop_at("prologue")
      for p in range(NP):
          T = cfg.T[p]; pc0 = cfg.pc0[p]; npr = cfg.pass_prompt[p]; last = (p == NP - 1)
          pend = pc0 + npr
          b0 = pc0
          nch = npr // 64
          mark_phase(f'p{p}.load')
          dma(SP, d_rope, cosT[:, 0:T], rope_d[p, 0, :, 0:T], [], [r_rope])
          dma(SP, d_rope, sinT[:, 0:T], rope_d[p, 1, :, 0:T], [], [r_rope])
          for (c0, n) in [(c, min(128, T - c)) for c in range(0, T, 128)]:
              for pi in range(KD // KH):
                  si = st["stg"] % 2
                  st["stg"] += 1
                  dma(SP, d_stg[si], stg[si][0:n, :], xin[cfg.R0[p] + c0:cfg.R0[p] + c0 + n, pi * KH * 128:(pi + 1) * KH * 128],
                      [], [r_stg[si]])
                  pt, rbs = next_dt()
                  for i in range(KH):
                      op(PE, lambda h, i=i, si=si, pt=pt, n=n: h.transpose(out=pt[:, i * 128:i * 128 + n],
                                                                          in_=stg[si][0:n, i * 128:(i + 1) * 128],
                                                                          identity=ident[0:n, 0:n]),
                         [r_stg[si], r_const], [rbs[(i * 128) // 512]])
                  op(ACT, lambda h, pt=pt, pi=pi, c0=c0, n=n: h.activation(
                      out=xT[:, pi * KH:(pi + 1) * KH, c0:c0 + n],
                      in_=pt[:, 0:KH * 128].rearrange("p (a b) -> p a b", b=128)[:, :, 0:n], func=AF.Copy),
                     rbs, [r_x[k] for k in range(pi * KH, (pi + 1) * KH)])
          stop_at('load')
          for l in range(LA):
              mark_phase(f'p{p}.A{l}.win')
              norm_pre(f"mixg{l}", 0, T)
              for j in range(KD):
                  pB, rbB = proj(1, lambda kc: hT[:, kc, :], lambda kc: r_h[kc], 0, T)
                  Bs, r_Bs = next_tmp()
                  op(ACT, lambda h: h.activation(out=Bs[:, 0:T], in_=pB[:, 0:T], func=AF.Copy), rbB, [r_Bs])
                  pC, rbC = proj(1, lambda kc: hT[:, kc, :], lambda kc: r_h[kc], 0, T)
                  Cs, r_Cs = next_tmp()
                  op(ACT, lambda h: h.activation(out=Cs[:, 0:T], in_=pC[:, 0:T], func=AF.Copy), rbC, [r_Cs])
                  if j == 0:
                      norm_post(0, T, need2=True)
                  pU, rbU = proj(1, lambda kc: hT[:, kc, :], lambda kc: r_h[kc], 0, T)
                  zb, r_zb = next_tmp()
                  op(DVE, lambda h: h.tensor_tensor(out=zb[:, 2:2 + T], in0=pU[:, 0:T], in1=Cs[:, 0:T], op=ALU.mult),
                     rbU + [r_Cs], [r_zb])
                  op(DVE, lambda h: h.tensor_tensor(out=zb[:, 2:2 + T], in0=zb[:, 2:2 + T], in1=rstd2[:, 0:T], op=ALU.mult),
                     [r_rstd2], [r_zb])
                  gs = 0
                  op(ACT, lambda h, gs=gs: h.activation(out=zb[:, 2 + gs:2 + gs + 2], in_=zprev[:, l, j, :], func=AF.Copy),
                     [r_zprev], [r_zb])
                  if last:
                      s0 = cfg.sc0[p][0] - 2
                      op(ACT, lambda h, s0=s0: h.activation(
                          out=zb[:, 2 + s0:2 + s0 + NSEQ * (SL + 2)].rearrange("p (s t) -> p s t", t=SL + 2)[:, :, 0:2],
                          in_=scv[:, l, j, :, :], func=AF.Copy), [r_const], [r_zb])
                  cvt, r_cv = next_tmp()
                  op(DVE, lambda h: h.tensor_scalar(out=cvt[:, 0:T], in0=zb[:, 2:2 + T], scalar1=pcol(f"cw{l}_2", j),
                                                    scalar2=None, op0=ALU.mult), [r_zb, r_const], [r_cv])
                  op(DVE, lambda h: h.scalar_tensor_tensor(out=cvt[:, 0:T], in0=zb[:, 1:1 + T], scalar=pcol(f"cw{l}_1", j),
                                                           in1=cvt[:, 0:T], op0=ALU.mult, op1=ALU.add), [r_zb, r_const], [r_cv])
                  op(DVE, lambda h: h.scalar_tensor_tensor(out=cvt[:, 0:T], in0=zb[:, 0:T], scalar=pcol(f"cw{l}_0", j),
                                                           in1=cvt[:, 0:T], op0=ALU.mult, op1=ALU.add), [r_zb, r_const], [r_cv])
                  op(ACT, lambda h: h.activation(out=zprev[:, l, j, :], in_=zb[:, 2 + pend - 2:2 + pend], func=AF.Copy),
                     [r_zb], [r_zprev])
                  if last:
                      op(ACT, lambda h: h.activation(out=zo[:, l, j, 0, :], in_=zb[:, 2 + pend - 2:2 + pend], func=AF.Copy),
                         [r_zb], [r_zo])
                      s0 = cfg.sc0[p][0] - 2
                      op(ACT, lambda h, s0=s0: h.activation(
                          out=zo[:, l, j, 1:1 + NSEQ, :],
                          in_=zb[:, 2 + s0:2 + s0 + NSEQ * (SL + 2)].rearrange("p (s t) -> p s t", t=SL + 2)[:, :, SL:SL + 2],
                          func=AF.Copy), [r_zb], [r_zo])
                  op(DVE, lambda h: h.tensor_tensor(out=cvt[:, 0:T], in0=cvt[:, 0:T], in1=Bs[:, 0:T], op=ALU.mult),
                     [r_Bs], [r_cv])
                  op(DVE, lambda h: h.tensor_tensor(out=gT[:, j, 0:T], in0=cvt[:, 0:T], in1=rstd[:, 0:T], op=ALU.mult),
                     [r_cv, r_rstd], [r_g[j]])
                  free_tmp(Bs, Cs, zb, cvt)
              mark_phase(f'p{p}.A{l}.wout')
              for n in range(KD):
                  pt, rb = proj(1, lambda kc: gT[:, kc, :], lambda kc: r_g[kc], 0, T)
                  resid_evac(pt, rb, n, 0, T)
              mlp(l, 0, T)
          if last:
              with nc.allow_non_contiguous_dma(reason="conv state output, tiny"):
                  for l in range(LA):
                      for s in range(1 + NSEQ):
                          for t in range(2):
                              dma(SP, d_zo, zout_d[l, s, t, :].rearrange("(k p) -> p k", p=128), zo[:, l, :, s, t],
                                  [r_zo], [r_out])
          stop_at('A')
          k0 = 6 if p == 0 else pc0
          mark_phase(f'p{p}.kv')
          koff = 128 - pc0
          if p > 0:
              src0 = (128 - cfg.pc0[p - 1]) + cfg.pc0[p - 1] + cfg.pass_prompt[p - 1] - 128
              for g in range(NG):
                  op(ACT, lambda h, g=g: h.activation(out=kTa[:, g, 0:128], in_=kTa[:, g, src0:src0 + 128], func=AF.Copy),
                     [r_kTa[g]], [r_kTa[g]])
                  op(ACT, lambda h, g=g: h.activation(out=vTa[:, g, 0:128], in_=vTa[:, g, src0:src0 + 128], func=AF.Copy),
                     [r_vTa[g]], [r_vTa[g]])
          norm_pre("kvg", k0, T)
          def k_after(g, kf, r_kf):
              if not last:
                  return
              ptk, rbk = next_bank()
              op(PE, lambda h: h.transpose(out=ptk[:, 0:128], in_=kf[:, pend - 128 - k0:pend - k0], identity=ident),
                 [r_kf, r_const], [rbk])
              op(ACT, lambda h: h.activation(out=kst[:, g, :], in_=ptk[:, 0:64], func=AF.Copy), [rbk], [r_kst])
              dma(SP, d_kst, ckp_d[:, g * 64:(g + 1) * 64], kst[:, g, :], [r_kst], [r_out])
              for s in range(NSEQ):
                  c0s = cfg.sc0[p][s]
                  ptk2, rbk2 = next_bank()
                  op(PE, lambda h, ptk2=ptk2, c0s=c0s: h.transpose(out=ptk2[0:SL, 0:128], in_=kf[:, c0s - k0:c0s - k0 + SL],
                                                                   identity=ident), [r_kf, r_const], [rbk2])
                  op(ACT, lambda h, ptk2=ptk2: h.activation(out=kst[0:SL, g, :], in_=ptk2[0:SL, 0:64], func=AF.Copy),
                     [rbk2], [r_kst])
                  dma(SP, d_kst, cks_d[s, 128 - SL:128, g * 64:(g + 1) * 64], kst[0:SL, g, :], [r_kst], [r_out])
          for _ in headnorm_pipeline(NG, pcol("gk"), k0, T, lambda g: kTa[:, g, koff + k0:koff + T], lambda g: r_kTa[g],
                                     want_f32=True, hook=lambda: norm_post(k0, T, need2=False), after=k_after):
              pass
          WV = T - k0
          for g in range(NG):
              pt, rb = proj(1, lambda kc: hT[:, kc, :], lambda kc: r_h[kc], k0, T)
              if last:
                  vT, r_vT = next_tmp()
                  op(DVE, lambda h, pt=pt, vT=vT: h.tensor_tensor(out=vT[:, 0:WV], in0=pt[:, 0:WV], in1=rstd[:, k0:T], op=ALU.mult),
                     rb + [r_rstd], [r_vT])
                  op(ACT, lambda h, g=g, vT=vT: h.activation(out=vTa[:, g, koff + k0:koff + T], in_=vT[:, 0:WV], func=AF.Copy),
                     [r_vT], [r_vTa[g]])
                  ptv, rbv_ = next_bank()
                  op(PE, lambda h, vT=vT, ptv=ptv: h.transpose(out=ptv[:, 0:128], in_=vT[:, pend - 128 - k0:pend - k0], identity=ident),
                     [r_vT, r_const], [rbv_])
                  op(ACT, lambda h, ptv=ptv, g=g: h.activation(out=kst[:, g, :], in_=ptv[:, 0:64], func=AF.Copy), [rbv_], [r_kst])
                  dma(SP, d_kst, cvp_d[:, g * 64:(g + 1) * 64], kst[:, g, :], [r_kst], [r_out])
                  for s in range(NSEQ):
                      c0s = cfg.sc0[p][s]
                      ptv2, rbv2 = next_bank()
                      op(PE, lambda h, vT=vT, ptv2=ptv2, c0s=c0s: h.transpose(out=ptv2[0:SL, 0:128], in_=vT[:, c0s - k0:c0s - k0 + SL],
                                                                             identity=ident), [r_vT, r_const], [rbv2])
                      op(ACT, lambda h, ptv2=ptv2, g=g: h.activation(out=kst[0:SL, g, :], in_=ptv2[0:SL, 0:64], func=AF.Copy),
                         [rbv2], [r_kst])
                      dma(SP, d_kst, cvs_d[s, 128 - SL:128, g * 64:(g + 1) * 64], kst[0:SL, g, :], [r_kst], [r_out])
                  free_tmp(vT)
              else:
                  op(DVE, lambda h, pt=pt, g=g: h.tensor_tensor(out=vTa[:, g, koff + k0:koff + T], in0=pt[:, 0:WV],
                                                                in1=rstd[:, k0:T], op=ALU.mult), rb + [r_rstd], [r_vTa[g]])
          identb = cmb[:, 0, :]
          vt_list = [(m + 2, 128 + 64 * m, 128 if m < nch - 1 else 64) for m in range(-2, nch)] + \
                    [(nch + 2 + s, koff + c0s, SL) for s, c0s in enumerate(cfg.sc0[p])]
          for (vi, i0, n) in vt_list:
              pvf, rbv = next_bank()
              pv = pvf.bitcast(BF16)
              for g in range(NG):
                  op(PE, lambda h, g=g, i0=i0, n=n, pv=pv: h.transpose(out=pv[0:n, g * 128:(g + 1) * 128],
                                                                      in_=vTa[:, g, i0:i0 + n], identity=identb),
                     [r_vTa[g], r_const], [rbv])
              op(ACT, lambda h, vi=vi, n=n, pv=pv: h.activation(out=Vt[0:n, vi, :], in_=pv[0:n, 0:NG * 128], func=AF.Copy),
                 [rbv], [r_Vt[vi]])
          stop_at('KV')
          for lb in range(LB):
              l = LA + lb
              mark_phase(f'p{p}.B{lb}.q')
              norm_pre(f"mixg{l}", b0, T)
              qgen = headnorm_pipeline(KD, pcol(f"gq{lb}"), b0, T, lambda n: qT[:, n, b0:T], lambda n: r_q[n],
                                       hook=lambda: norm_post(b0, T, need2=False))
              qstate = {"done": 0}
              def q_advance(upto):
                  while qstate["done"] < min(upto, KD + 2):
                      next(qgen)
                      qstate["done"] += 1
              mark_phase(f'p{p}.B{lb}.attn')
              op(ACT, lambda h, lb=lb: h.activation(
                  out=eskb[0:1, :].rearrange("p (a q) -> p a q", q=64),
                  in_=esk[0:1, lb * NG * 8:(lb + 1) * NG * 8].unsqueeze(2).to_broadcast([1, NG * 8, 64]), func=AF.Copy),
                 [r_const], [r_eskb])
              for g in range(NG):
                  units = []
                  koff = 128 - pc0
                  for c in range(nch):
                      biasA = None
                      if p == 0 and c < 2:
                          biasA = pcs[:, off["kmask"] + c:off["kmask"] + c + 1]
                      kb = [(kTa[:, g, 64 * c:64 * c + 128], r_kTa[g], Vt[:, c, g * 128:(g + 1) * 128], r_Vt[c], 128, biasA),
                            (kTa[:, g, 128 + 64 * c:192 + 64 * c], r_kTa[g], Vt[0:64, c + 2, g * 128:(g + 1) * 128], r_Vt[c + 2],
                             64, None)]
                      units.append((lb, g, pc0 + 64 * c, 64, kb))
                  for s, c0s in enumerate(cfg.sc0[p]):
                      kb = [(ckT[:, s, g, :], r_ckT, cvS[:, s, g * 128:(g + 1) * 128], r_cvS, 128, None),
                            (kTa[:, g, koff + c0s:koff + c0s + SL], r_kTa[g], Vt[0:SL, nch + 2 + s, g * 128:(g + 1) * 128],
                             r_Vt[nch + 2 + s], SL, None)]
                      units.append((lb, g, c0s, SL, kb))
                  q_advance(4 * g + 6 if INTERLEAVE_QA else KD + 2)
                  for ui, _ in enumerate(attn_pipeline(units)):
                      if INTERLEAVE_QA and ui % 3 == 2:
                          q_advance(qstate["done"] + 1)
              q_advance(KD + 2)
              stop_at('ATT')
              if last:
                  pass
              mark_phase(f'p{p}.B{lb}.wo')
              for n in range(KD):
                  pt, rb = proj(1, lambda kc: attT[:, kc, :], lambda kc: r_at[kc], b0, T)
                  resid_evac(pt, rb, n, b0, T)
              mlp(l, b0, T)
          stop_at('B')
          mark_phase(f'p{p}.store')
          for (c0, n) in [(c, min(128, T - c)) for c in range(b0, T, 128)]:
              for pi in range(KD // KH):
                  pt, rbs = next_dt()
                  for i in range(KH):
                      kc = pi * KH + i
                      op(PE, lambda h, i=i, kc=kc, pt=pt, c0=c0, n=n: h.transpose(out=pt[0:n, i * 128:(i + 1) * 128],
                                                                                 in_=xT[:, kc, c0:c0 + n], identity=ident),
                         [r_x[kc], r_const], [rbs[(i * 128) // 512]])
                  si = st["stg"] % 2
                  st["stg"] += 1
                  op(ACT, lambda h, pt=pt, si=si, n=n: h.activation(out=stg[si][0:n, :], in_=pt[0:n, 0:KH * 128], func=AF.Copy),
                     rbs, [r_stg[si]])
                  ro = cfg.O0[p] + c0 - b0
                  dma(SP, d_stg[si], yout[ro:ro + n, pi * KH * 128:(pi + 1) * KH * 128], stg[si][0:n, :], [r_stg[si]], [r_out])
    except _Stop:
        pass
    for s in range(NSEQ):
        dma(SP, d_cp, cks_d[s, 0:128 - SL, :].rearrange("r (g c) -> r g c", c=64),
            ckd_d[s, SL:128, :].rearrange("r (g c) -> r g c", c=128)[:, :, 0:64], [], [r_out])
        dma(SP, d_cp, cvs_d[s, 0:128 - SL, :].rearrange("r (g c) -> r g c", c=64),
            cvd_d[s, SL:128, :].rearrange("r (g c) -> r g c", c=128)[:, :, 0:64], [], [r_out])
    for ds in (d_stg[0], d_stg[1], d_kst, d_vst, d_zo, d_cp):
        if ds.cnt:
            nc.sync.wait_ge(ds.sem, ds.cnt)
    assert getattr(cfg, 'stop', None) or ws["next_use"] == total_units, (ws["next_use"], total_units)
    mark_phase('end')
    cfg.marks = marks
    es.close()
    return nc


def assemble(cfg, results):
    D, NG, NSEQ, SL = cfg.D, cfg.NG, cfg.NSEQ, cfg.SL
    nco = cfg.NCORES
    yp = np.zeros((1, nco * cfg.PC, D), np.float32)
    ys = np.zeros((nco * NSEQ, SL, D), np.float32)
    scs = np.zeros((cfg.LA, nco * NSEQ, 2, D), np.float32)
    cks = np.zeros((nco * NSEQ, 128, NG, 64), np.float32)
    cvs = np.zeros((nco * NSEQ, 128, NG, 64), np.float32)
    for c in range(nco):
        r = results[c]
        tok = 0
        for p in range(cfg.NP):
            n = cfg.pass_prompt[p]
            yp[0, c * cfg.PC + tok:c * cfg.PC + tok + n] = r["yout"][cfg.O0[p]:cfg.O0[p] + n]
            tok += n
            for s, c0 in enumerate(cfg.sc0[p]):
                o = cfg.O0[p] + c0 - cfg.pc0[p]
                ys[c * NSEQ + s] = r["yout"][o:o + SL]
        scs[:, c * NSEQ:(c + 1) * NSEQ] = r["zout"][:, 1:]
        cks[c * NSEQ:(c + 1) * NSEQ] = r["cks"].reshape(NSEQ, 128, NG, 64)
        cvs[c * NSEQ:(c + 1) * NSEQ] = r["cvs"].reshape(NSEQ, 128, NG, 64)
    rl = results[nco - 1]
    scp = np.ascontiguousarray(rl["zout"][:, 0:1])
    ckp = rl["ckp"].reshape(1, 128, NG, 64)
    cvp = rl["cvp"].reshape(1, 128, NG, 64)
    return (yp, ys, scp, scs, ckp, cvp, cks, cvs)


def run(cfg, inp):
    inp = {k: np.asarray(v, dtype=np.float32) for k, v in inp.items()}
    maps = host_prepare(cfg, inp)
    nc = build_program(cfg)
    res = run_bass_kernel_spmd(nc, maps, core_ids=list(range(cfg.NCORES)))
    return assemble(cfg, res.results)


def kernel(**inputs):
    return run(Cfg(), inputs)
```
